# Optimizing a Trainium2 kernel written in Bass

```python
import math
import jax
import jax.numpy as jnp
from jax import lax
import numpy as np


D_MODEL = 1024
BATCH = 8
SEQ = 4096
DEPTH = 2

EPS = 1e-6
CONV_WIDTH = 4
MIX_WIDTH = 2 * D_MODEL
ATT_HEAD_DIM = 64
ATT_WIDTH = MIX_WIDTH // 4
ATT_HEADS = ATT_WIDTH // ATT_HEAD_DIM
Q_BLOCK = 128
SSD_HEAD_DIM = 64
SSD_WIDTH = MIX_WIDTH // 2
SSD_HEADS = SSD_WIDTH // SSD_HEAD_DIM
SSD_GROUPS = 2
SSD_STATE = 128
SSD_CHUNK = 128
SSD_CONV_DIM = SSD_WIDTH + 2 * SSD_GROUPS * SSD_STATE
DT_MIN = 0.001
DT_MAX = 0.1
LRU_WIDTH = MIX_WIDTH // 4
LRU_BLOCKS = 8
LRU_BLOCK_DIM = LRU_WIDTH // LRU_BLOCKS
LRU_C = 8.0
IN_SIZES = (ATT_WIDTH, ATT_WIDTH, ATT_WIDTH, ATT_WIDTH, ATT_HEADS,
            SSD_WIDTH, SSD_CONV_DIM, SSD_HEADS,
            LRU_WIDTH, LRU_WIDTH)
D_IN = sum(IN_SIZES)

kernel_name = 'hymba_fox_ssd_rglru_hybrid'


def rms_norm(x, g):
    xf = x.astype(jnp.float32)
    return xf * lax.rsqrt(jnp.mean(xf * xf, axis=-1, keepdims=True) + EPS) * g


def causal_dwconv(x, w, b):
    k = w.shape[0]
    y = lax.conv_general_dilated(
        x, w.astype(x.dtype)[:, None, :], window_strides=(1,),
        padding=((k - 1, 0),), dimension_numbers=('NWC', 'WIO', 'NWC'),
        feature_group_count=x.shape[-1])
    return y + b


def fox_attention(q, k, v, log_f, q_g, k_g):
    bsz, seq = q.shape[0], q.shape[1]
    nb = seq // Q_BLOCK
    q = rms_norm(q, q_g)
    k = rms_norm(k, k_g)
    c = jnp.cumsum(log_f.astype(jnp.float32), axis=1)
    qb = q.reshape(bsz, nb, Q_BLOCK, ATT_HEADS, ATT_HEAD_DIM).transpose(1, 0, 3, 2, 4)
    cq = c.reshape(bsz, nb, Q_BLOCK, ATT_HEADS).transpose(1, 0, 3, 2)
    kt = k.transpose(0, 2, 1, 3)
    vt = v.transpose(0, 2, 1, 3)
    ck = c.transpose(0, 2, 1)
    kpos = jnp.arange(seq)
    scale = ATT_HEAD_DIM ** -0.5

    def one_block(args):
        q_i, c_i, blk = args
        logits = jnp.einsum('bhqd,bhkd->bhqk', q_i, kt) * scale + (c_i[..., :, None] - ck[:, :, None, :])
        qpos = blk * Q_BLOCK + jnp.arange(Q_BLOCK)
        logits = jnp.where(kpos[None, :] <= qpos[:, None], logits, -jnp.inf)
        p = jax.nn.softmax(logits.astype(jnp.float32), axis=-1)
        return jnp.einsum('bhqk,bhkd->bhqd', p, vt)

    o = lax.map(one_block, (qb, cq, jnp.arange(nb)))
    return o.transpose(1, 0, 3, 2, 4).reshape(bsz, seq, ATT_WIDTH)


def ssd_mixer(xbc_raw, dt_raw, z, conv_w, conv_b, dt_bias, a_log, d_skip, norm_g):
    bsz, seq = xbc_raw.shape[0], xbc_raw.shape[1]
    nc = seq // SSD_CHUNK
    e = SSD_HEADS // SSD_GROUPS
    gn = SSD_GROUPS * SSD_STATE
    xbc = jax.nn.silu(causal_dwconv(xbc_raw, conv_w, conv_b))
    xs, bm, cm = jnp.split(xbc, [SSD_WIDTH, SSD_WIDTH + gn], axis=-1)
    dt = jax.nn.softplus(dt_raw + dt_bias)
    a = -jnp.exp(a_log.astype(jnp.float32))
    xh = xs.reshape(bsz, seq, SSD_HEADS, SSD_HEAD_DIM)
    xdt = (xh * dt[..., None]).reshape(bsz, nc, SSD_CHUNK, SSD_GROUPS, e, SSD_HEAD_DIM)
    adt = (dt * a).reshape(bsz, nc, SSD_CHUNK, SSD_GROUPS, e).transpose(0, 1, 3, 4, 2)
    bc = bm.reshape(bsz, nc, SSD_CHUNK, SSD_GROUPS, SSD_STATE)
    cc = cm.reshape(bsz, nc, SSD_CHUNK, SSD_GROUPS, SSD_STATE)
    a_cs = jnp.cumsum(adt, axis=-1)
    causal = jnp.tril(jnp.ones((SSD_CHUNK, SSD_CHUNK), dtype=bool))
    seg = a_cs[..., :, None] - a_cs[..., None, :]
    lmat = jnp.exp(jnp.where(causal, seg, -jnp.inf))
    cb = jnp.einsum('bclgn,bcsgn->bcgls', cc, bc)
    y_diag = jnp.einsum('bcgels,bcsgep->bclgep', cb[:, :, :, None] * lmat, xdt)
    decay_states = jnp.exp(a_cs[..., -1:] - a_cs)
    states = jnp.einsum('bclgn,bcgel,bclgep->bcgepn', bc, decay_states, xdt)
    tot = jnp.concatenate([jnp.zeros_like(a_cs[:, :1, ..., -1]), a_cs[..., -1]], axis=1)
    tot_cs = jnp.cumsum(tot, axis=1)
    seg_c = tot_cs[:, :, None] - tot_cs[:, None, :]
    causal_c = jnp.tril(jnp.ones((nc + 1, nc + 1), dtype=bool))[None, :, :, None, None]
    decay_chunk = jnp.exp(jnp.where(causal_c, seg_c, -jnp.inf))
    states_pad = jnp.concatenate([jnp.zeros_like(states[:, :1]), states], axis=1)
    new_states = jnp.einsum('bzcge,bcgepn->bzgepn', decay_chunk, states_pad)
    prev_states = new_states[:, :nc]
    y_off = jnp.einsum('bclgn,bcgepn,bcgel->bclgep', cc, prev_states, jnp.exp(a_cs))
    y = (y_diag + y_off).reshape(bsz, seq, SSD_HEADS, SSD_HEAD_DIM) + xh * d_skip[:, None]
    y = y.reshape(bsz, seq, SSD_WIDTH)
    gated = (y * jax.nn.silu(z)).reshape(bsz, seq, SSD_GROUPS, SSD_WIDTH // SSD_GROUPS)
    gated = rms_norm(gated, 1.0).reshape(bsz, seq, SSD_WIDTH)
    return gated * norm_g


def _lin_combine(c1, c2):
    a1, b1 = c1
    a2, b2 = c2
    return a1 * a2, a2 * b1 + b2


def rglru_mixer(x_raw, conv_w, conv_b, w_a, b_a, w_x, b_x, lam):
    bsz, seq = x_raw.shape[0], x_raw.shape[1]
    xc = causal_dwconv(x_raw, conv_w, conv_b)
    xb = xc.reshape(bsz, seq, LRU_BLOCKS, LRU_BLOCK_DIM)
    r = jax.nn.sigmoid(jnp.einsum('bsnd,nde->bsne', xb, w_a).reshape(bsz, seq, LRU_WIDTH) + b_a)
    i = jax.nn.sigmoid(jnp.einsum('bsnd,nde->bsne', xb, w_x).reshape(bsz, seq, LRU_WIDTH) + b_x)
    log_a = (-LRU_C * r * jax.nn.softplus(-lam)).astype(jnp.float32)
    a = jnp.exp(log_a)
    mult = jnp.sqrt(-jnp.expm1(2.0 * log_a))
    mult = jnp.where((jnp.arange(seq) == 0)[None, :, None], 1.0, mult)
    b = mult * (i * xc)
    _, h = lax.associative_scan(_lin_combine, (a, b), axis=1)
    return h


def setup_inputs(seed: int = 0) -> dict:
    key = jax.random.key(seed)
    ks = jax.random.split(key, 24)
    f32 = jnp.float32
    L = DEPTH
    nrm = lambda k, s: jax.random.normal(k, s, f32)
    x = nrm(ks[0], (BATCH, SEQ, D_MODEL))
    norm_g = 1.0 + 0.05 * nrm(ks[1], (L, D_MODEL))
    w_in = nrm(ks[2], (L, D_MODEL, D_IN)) * D_MODEL ** -0.5
    q_norm_g = 1.0 + 0.05 * nrm(ks[3], (L, ATT_HEAD_DIM))
    k_norm_g = 1.0 + 0.05 * nrm(ks[4], (L, ATT_HEAD_DIM))
    forget_b = jax.random.uniform(ks[5], (L, ATT_HEADS), f32, minval=1.0, maxval=4.0)
    ssd_conv_w = nrm(ks[6], (L, CONV_WIDTH, SSD_CONV_DIM)) * CONV_WIDTH ** -0.5
    ssd_conv_b = 0.02 * nrm(ks[7], (L, SSD_CONV_DIM))
    dt0 = jnp.exp(jax.random.uniform(ks[8], (L, SSD_HEADS), f32) * (math.log(DT_MAX) - math.log(DT_MIN)) + math.log(DT_MIN))
    ssd_dt_bias = dt0 + jnp.log(-jnp.expm1(-dt0))
    ssd_a_log = jnp.log(jax.random.uniform(ks[9], (L, SSD_HEADS), f32, minval=1.0, maxval=16.0))
    ssd_d = 1.0 + 0.1 * nrm(ks[10], (L, SSD_HEADS))
    ssd_norm_g = 1.0 + 0.05 * nrm(ks[11], (L, SSD_WIDTH))
    lru_conv_w = nrm(ks[12], (L, CONV_WIDTH, LRU_WIDTH)) * CONV_WIDTH ** -0.5
    lru_conv_b = 0.02 * nrm(ks[13], (L, LRU_WIDTH))
    lru_w_a = nrm(ks[14], (L, LRU_BLOCKS, LRU_BLOCK_DIM, LRU_BLOCK_DIM)) * LRU_BLOCK_DIM ** -0.5
    lru_b_a = 0.02 * nrm(ks[15], (L, LRU_WIDTH))
    lru_w_x = nrm(ks[16], (L, LRU_BLOCKS, LRU_BLOCK_DIM, LRU_BLOCK_DIM)) * LRU_BLOCK_DIM ** -0.5
    lru_b_x = 0.02 * nrm(ks[17], (L, LRU_WIDTH))
    a_pow_c = jax.random.uniform(ks[18], (L, LRU_WIDTH), f32, minval=0.9, maxval=0.999)
    s = a_pow_c ** (1.0 / LRU_C)
    lru_lambda = jnp.log(s) - jnp.log1p(-s)
    w_out = nrm(ks[19], (L, MIX_WIDTH, D_MODEL)) * MIX_WIDTH ** -0.5
    return {'x': x, 'norm_g': norm_g, 'w_in': w_in, 'q_norm_g': q_norm_g, 'k_norm_g': k_norm_g,
            'forget_b': forget_b, 'ssd_conv_w': ssd_conv_w, 'ssd_conv_b': ssd_conv_b,
            'ssd_dt_bias': ssd_dt_bias, 'ssd_a_log': ssd_a_log, 'ssd_d': ssd_d,
            'ssd_norm_g': ssd_norm_g, 'lru_conv_w': lru_conv_w, 'lru_conv_b': lru_conv_b,
            'lru_w_a': lru_w_a, 'lru_b_a': lru_b_a, 'lru_w_x': lru_w_x, 'lru_b_x': lru_b_x,
            'lru_lambda': lru_lambda, 'w_out': w_out}


def reference(x, norm_g, w_in, q_norm_g, k_norm_g, forget_b, ssd_conv_w, ssd_conv_b,
              ssd_dt_bias, ssd_a_log, ssd_d, ssd_norm_g, lru_conv_w, lru_conv_b,
              lru_w_a, lru_b_a, lru_w_x, lru_b_x, lru_lambda, w_out):
    out_dtype = x.dtype
    h = x.astype(jnp.float32)
    bsz, seq = h.shape[0], h.shape[1]
    split_points = np.cumsum(IN_SIZES)[:-1].tolist()
    for l in range(DEPTH):
        u = rms_norm(h, norm_g[l])
        proj = jnp.einsum('bsd,de->bse', u, w_in[l])
        (q, k, v, z_att, f_raw, z_ssd, xbc, dt_raw, x_lru, z_lru) = jnp.split(proj, split_points, axis=-1)
        hs = (bsz, seq, ATT_HEADS, ATT_HEAD_DIM)
        log_f = jax.nn.log_sigmoid(f_raw + forget_b[l])
        y_att = fox_attention(q.reshape(hs), k.reshape(hs), v.reshape(hs), log_f,
                              q_norm_g[l], k_norm_g[l]) * jax.nn.silu(z_att)
        y_ssd = ssd_mixer(xbc, dt_raw, z_ssd, ssd_conv_w[l], ssd_conv_b[l], ssd_dt_bias[l],
                          ssd_a_log[l], ssd_d[l], ssd_norm_g[l])
        y_lru = rglru_mixer(x_lru, lru_conv_w[l], lru_conv_b[l], lru_w_a[l], lru_b_a[l],
                            lru_w_x[l], lru_b_x[l], lru_lambda[l]) * jax.nn.silu(z_lru)
        mix = jnp.concatenate([y_att, y_ssd, y_lru], axis=-1)
        h = h + jnp.einsum('bse,ed->bsd', mix, w_out[l])
    return h.astype(out_dtype)
```

```python
import contextlib
import numpy as np
import concourse.bass as bass
import concourse.mybir as mybir
from concourse.bass_utils import run_bass_kernel_spmd

F32 = mybir.dt.float32
BF16 = mybir.dt.bfloat16
ALU = mybir.AluOpType
AF = mybir.ActivationFunctionType

ENGS = ("pe", "act", "dve", "pool", "sp")

S = 4096
D = 1024
DIN = 5656
MIX = 2048
NT = 8
EPS = 1e-6
NEG = -30000.0

PP_G, PP_GQ, PP_GK, PP_FB, PP_DTB, PP_ALOG = 0, 8, 9, 10, 11, 12
PP_BA, PP_BX, PP_LAM = 13, 17, 21
PP_SCW, PP_SCB, PP_LCW, PP_LCB, PP_DBC, PP_GN = 25, 73, 85, 101, 105, 121
NPP = 1145
CC_ID, CC_TRI, CC_MNEG, CC_BD, CC_SELH, CC_SELR = 0, 128, 256, 768, 896, 1920
CC_ZERO = 3968
NCC = 4096


class _Op:
    __slots__ = ("eng", "fn", "reads", "writes", "deps", "is_dma", "signaled",
                 "sem", "semval", "idx", "clock", "nop", "seg", "cost", "lat", "aset")

    def __init__(self, eng, fn, reads, writes, is_dma):
        self.eng = eng
        self.fn = fn
        self.reads = reads
        self.writes = writes
        self.is_dma = is_dma
        self.deps = []
        self.signaled = False
        self.sem = None
        self.semval = None
        self.clock = None
        self.nop = False


class _FakeEng:
    def __init__(self, eng, is_dma):
        self.eng = eng
        self.is_dma = is_dma
        self.cost = 0.0
        self.lat = 0.0
        self.aset = None

    def __getattr__(self, name):
        def f(*a, **k):
            out = k.get("out", a[0] if a else None)
            try:
                n = float(out.free_size())
            except Exception:
                n = 64.0
            if name == "dma_start":
                nb_ = float(out.nbytes()) if hasattr(out, "nbytes") else 4096.0
                self.cost += 150.0 if self.eng == "sp" else 900.0
                self.lat = 2500.0 + nb_ * 128.0 / 150.0 / 128.0
                return self
            if name == "matmul":
                r = k.get("rhs")
                n = float(r.free_size())
                self.cost += max(n, 64.0) / 2.2 + 12.0
                self.lat = 250.0
                return self
            if name == "transpose":
                r = k.get("in_")
                self.cost += max(float(r.free_size()), 64.0) / 2.2 + 12.0
                self.lat = 250.0
                return self
            if name == "activation":
                fn_ = str(k.get("func"))
                for key_, set_ in (("Exp", "le"), ("Ln", "le"), ("Silu", "si"), ("Tanh", "si"), ("Sigmoid", "sg"), ("Sqrt", "sq")):
                    if fn_.endswith(key_):
                        self.aset = set_
                self.cost += (224.0 + n) / 1.2 + (90.0 if k.get("accum_out") is not None else 0.0)
                self.lat = 100.0
                return self
            fac = {"tensor_copy": 0.6, "tensor_scalar": 0.6, "memset": 0.5, "tensor_tensor": 1.0,
                   "scalar_tensor_tensor": 1.0, "tensor_tensor_scan": 2.0, "reciprocal": 3.8}.get(name, 1.0)
            c = (70.0 + n * fac) / 0.96
            if self.eng == "pool":
                c *= 1.7
            self.cost += c
            self.lat = 100.0
            return self
        return f


class Prog:
    def __init__(self, nc, n_dma_sems=10):
        self.nc = nc
        self.ops = []
        self.last_w = {}
        self.readers = {}
        self.n_dma_sems = n_dma_sems
        self.pending_dma = []
        self.nbar = 0
        self.bar_fns = {}
        self.seg = 0
        self.window = 350
        self.reorder = True

    def op(self, eng, fn, r=(), w=(), dma=False):
        o = _Op(eng, fn, tuple(r), tuple(w), dma)
        o.idx = len(self.ops)
        deps = set()
        for t in o.reads:
            x = self.last_w.get(t)
            if x is not None:
                deps.add(x)
        for t in o.writes:
            x = self.last_w.get(t)
            if x is not None:
                deps.add(x)
            for rr in self.readers.get(t, ()):
                deps.add(rr)
        for t in o.reads:
            self.readers.setdefault(t, []).append(o)
        for t in o.writes:
            self.last_w[t] = o
            self.readers[t] = []
        o.deps = sorted(deps, key=lambda d: d.idx)
        o.seg = self.seg
        self.ops.append(o)
        if dma:
            self.pending_dma.append(o)
        return o

    def barrier(self):
        n = self.nbar
        self.nbar += 1
        self.seg += 1
        toks = []
        for e in ("pe", "act", "dve", "pool"):
            fn, r_, w_ = self.bar_fns[e]
            self.op(e, fn, r_, list(w_) + [("bar", n, e)])
            toks.append(("bar", n, e))
        pend = list(self.pending_dma)
        for e in ENGS:
            o = self.op(e, lambda eng: None, toks, ())
            o.nop = True
            o.deps = o.deps + pend
        self.pending_dma = []
        self.last_w = {}
        self.readers = {}
        self.seg += 1

    def schedule(self):
        for o in self.ops:
            fk = _FakeEng(o.eng, o.is_dma)
            try:
                o.fn(fk)
            except Exception:
                fk.cost = 300.0
            o.cost = max(fk.cost, 20.0)
            o.lat = fk.lat
            o.aset = fk.aset
        segs = {}
        for o in self.ops:
            segs.setdefault(o.seg, []).append(o)
        order = []
        for sg in sorted(segs):
            ops = segs[sg]
            if len(ops) < 3 or any(o.nop for o in ops) or not self.reorder:
                order.extend(ops)
                continue
            inseg = set(id(o) for o in ops)
            users = {id(o): [] for o in ops}
            nrem = {}
            for o in ops:
                c = 0
                for d in o.deps:
                    if id(d) in inseg:
                        users[id(d)].append(o)
                        c += 1
                nrem[id(o)] = c
            t_free = {e: 0.0 for e in ENGS}
            cur_set = [None]
            fin = {}
            drt = {}
            ready = {e: [] for e in ENGS}
            for o in ops:
                if nrem[id(o)] == 0:
                    drt[id(o)] = 0.0
                    ready[o.eng].append(o)
            done = set()
            pos = 0
            nleft = len(ops)
            while nleft:
                while id(ops[pos]) in done:
                    pos += 1
                lim = ops[pos].idx + self.window
                best = None
                for e in ENGS:
                    tf = t_free[e]
                    for o in ready[e]:
                        if o.idx >= lim:
                            continue
                        est = drt[id(o)]
                        if est < tf:
                            est = tf
                        if o.aset is not None and o.aset != cur_set[0]:
                            est += 1300.0
                        key = (est, o.idx)
                        if best is None or key < best[0]:
                            best = (key, o)
                (est, _), o = best
                ready[o.eng].remove(o)
                if o.aset is not None:
                    cur_set[0] = o.aset
                if o.is_dma:
                    t_free[o.eng] = est + o.cost
                    fin[id(o)] = est + o.cost + o.lat
                else:
                    t_free[o.eng] = est + o.cost
                    fin[id(o)] = est + o.cost + o.lat
                done.add(id(o))
                order.append(o)
                nleft -= 1
                for u in users[id(o)]:
                    nrem[id(u)] -= 1
                    if nrem[id(u)] == 0:
                        t = 0.0
                        for d in u.deps:
                            if id(d) in inseg:
                                ft = fin[id(d)] + (120.0 if d.eng != u.eng else 0.0)
                                if d.eng == u.eng and not d.is_dma:
                                    ft = fin[id(d)] - d.lat
                                if ft > t:
                                    t = ft
                        drt[id(u)] = t
                        ready[u.eng].append(u)
        for i, o in enumerate(order):
            o.idx = i
        self.ops = order

    def pe(self, fn, r=(), w=()):
        return self.op("pe", fn, r, w)

    def act(self, fn, r=(), w=()):
        return self.op("act", fn, r, w)

    def dve(self, fn, r=(), w=()):
        return self.op("dve", fn, r, w)

    def pool(self, fn, r=(), w=()):
        return self.op("pool", fn, r, w)

    def dma(self, eng, fn, r=(), w=()):
        return self.op(eng, fn, r, w, dma=True)

    def emit(self):
        nc = self.nc
        self.schedule()
        ops = self.ops

        def need_wait(o, d):
            if d.nop:
                assert d.eng == o.eng
                return False
            if d.is_dma or d.eng != o.eng:
                return True
            if o.eng == "pe":
                return False
            return True

        for o in ops:
            o.deps = [d for d in o.deps if need_wait(o, d)]
            for d in o.deps:
                d.signaled = True
        per_eng = {e: [x for x in ops if x.eng == e] for e in ENGS}
        stack = contextlib.ExitStack()
        tick_sem = {}
        for e in ENGS:
            if any(x.signaled and not x.is_dma for x in per_eng[e]):
                tick_sem[e] = stack.enter_context(nc.semaphore("tk_" + e))
        for e in ENGS:
            dl = [x for x in per_eng[e] if x.is_dma]
            if not dl:
                continue
            n = min(self.n_dma_sems, len(dl))
            sems = [stack.enter_context(nc.semaphore("dq_%s_%d" % (e, i))) for i in range(n)]
            cnt = [0] * n
            prev = [None] * n
            for k, x in enumerate(dl):
                i = k % n
                cnt[i] += 1
                x.sem = sems[i]
                x.semval = 16 * cnt[i]
                x.signaled = True
                if prev[i] is not None:
                    x.deps = x.deps + [prev[i]]
                prev[i] = x
        for e in ENGS:
            t = 0
            for x in per_eng[e]:
                if x.is_dma or not x.signaled:
                    continue
                t += 1
                x.sem = tick_sem[e]
                x.semval = t
        known = {e: {} for e in ENGS}
        plan = {e: [] for e in ENGS}
        nw = 0
        for x in ops:
            kn = known[x.eng]
            need = {}
            for d in x.deps:
                key = id(d.sem)
                if kn.get(key, 0) >= d.semval:
                    continue
                need[key] = (d.sem, d.semval)
                kn[key] = d.semval
                if d.clock:
                    for k2, v2 in d.clock.items():
                        if kn.get(k2, 0) < v2:
                            kn[k2] = v2
            waits = [v for k_, v in need.items() if kn.get(k_, 0) <= v[1]]
            if x.signaled and not x.is_dma:
                c = dict(kn)
                c[id(x.sem)] = x.semval
                x.clock = c
            plan[x.eng].append((waits, x))
            nw += len(waits)
        self.stats = {e: len(per_eng[e]) for e in ENGS}
        self.stats["waits"] = nw

        def make(e):
            def body(eng):
                for waits, x in plan[e]:
                    for s_, v_ in waits:
                        eng.wait_ge(s_, v_)
                    ins = x.fn(eng)
                    if x.signaled:
                        ins.then_inc(x.sem, 16 if x.is_dma else 1)
            return body

        with stack:
            with nc.Block() as block:
                if plan["pe"]:
                    block.tensor(make("pe"))
                if plan["act"]:
                    block.scalar(make("act"))
                if plan["dve"]:
                    block.vector(make("dve"))
                if plan["pool"]:
                    block.gpsimd(make("pool"))
                if plan["sp"]:
                    block.sync(make("sp"))


def build_program(L=2, dbg=None, nchunks=NT):
    nc = bass.Bass("TRN2", target_bir_lowering=False)
    x_d = nc.dram_tensor("x", [S, D], F32, kind="ExternalInput").ap()
    win_d = nc.dram_tensor("w_in", [L, D, DIN], F32, kind="ExternalInput").ap()
    wout_d = nc.dram_tensor("w_out", [L, MIX, D], F32, kind="ExternalInput").ap()
    pp_d = nc.dram_tensor("pp", [L, 128, NPP], F32, kind="ExternalInput").ap()
    lw_d = nc.dram_tensor("lw", [L, 128, 8, 128], F32, kind="ExternalInput").ap()
    cc_d = nc.dram_tensor("cc", [128, NCC], F32, kind="ExternalInput").ap()
    out_d = nc.dram_tensor("out", [S, D], F32, kind="ExternalOutput").ap()
    hmid_d = nc.dram_tensor("hmid", [S, D], F32, kind="Internal").ap()
    utd_d = nc.dram_tensor("utd", [128, 8, S], BF16, kind="Internal").ap()
    yatd_d = nc.dram_tensor("yatd", [128, 4, S], BF16, kind="Internal").ap()
    ymd_d = nc.dram_tensor("ymd", [128, 12, S], BF16, kind="Internal").ap()
    wbf_d = nc.dram_tensor("wbf", [L, 128, 8, DIN], BF16, kind="Internal").ap()
    wobf_d = nc.dram_tensor("wobf", [L, 128, 16, D], BF16, kind="Internal").ap()
    dbg_out = {}
    st = contextlib.ExitStack()

    def mk(stk, sfx=""):
        def sb(name, shape, dt=F32):
            return stk.enter_context(nc.sbuf_tensor(name + sfx, shape, dt))
        return sb

    with st:
        P = Prog(nc)
        sb = mk(st)
        ccb = sb("ccb", [128, NCC], BF16)
        ccf = sb("ccf", [128, 256], F32)
        ppt = sb("ppt", [128, NPP], F32)
        lwb = sb("lwb", [128, 8, 128], BF16)
        small = sb("small", [128, 64], F32)
        onesrow = sb("onesrow", [16, 512], F32)
        scr = sb("scr", [128, 8], F32)
        ptr = st.enter_context(nc.psum_tensor("ptr", [128, 8, 128], BF16))
        pj = [st.enter_context(nc.psum_tensor("pj%d" % i, [128, 512], F32)) for i in range(2)]
        pm = st.enter_context(nc.psum_tensor("pm", [128, 512], F32))
        pa = [st.enter_context(nc.psum_tensor("pa%d" % i, [128, 512], F32)) for i in range(4)]

        ident = ccf[:, 0:128]
        identb = ccb[:, CC_ID:CC_ID + 128]
        trib = ccb[:, CC_TRI:CC_TRI + 128]
        mneg4 = ccb[:, CC_MNEG:CC_MNEG + 512]
        BDb = ccb[:, CC_BD:CC_BD + 128]
        zero128 = ccb[:, CC_ZERO:CC_ZERO + 128]
        selH = ccb[:, CC_SELH:CC_SELH + 1024].rearrange("p (h c) -> p h c", h=8)
        selR = ccb[:, CC_SELR:CC_SELR + 2048].rearrange("p (h c) -> p h c", h=16)
        epsc = small[:, 40:41]
        ptrfA = ptr[:, :, :].rearrange("p a b -> p (a b)").bitcast(F32)

        P.bar_fns = {
            "pe": (lambda e: e.matmul(pm[0:8, 0:8], lhsT=identb[0:8, 0:8], rhs=identb[0:8, 0:8], start=True, stop=True),
                   ["ccb"], ["pm"]),
            "act": (lambda e: e.activation(out=scr[:, 0:1], in_=scr[:, 1:2], func=AF.Copy), [], ["scr_a"]),
            "dve": (lambda e: e.memset(scr[:, 2:3], 0.0), [], ["scr_d"]),
            "pool": (lambda e: e.memset(scr[:, 4:5], 0.0), [], ["scr_p"]),
        }

        def dump(name, ap, shape, dt, rtoks):
            if dbg is None or name not in dbg:
                return
            t = nc.dram_tensor("dbg_" + name, list(shape), dt, kind="ExternalOutput").ap()
            dbg_out[name] = t
            P.dma("sp", lambda e: e.dma_start(out=t, in_=ap), r=rtoks, w=["dbg_" + name])

        P.dma("pool", lambda e: e.dma_start(out=ccb[:, :], in_=cc_d[:, :]), w=["ccb"])
        P.dma("sp", lambda e: e.dma_start(out=ccf[:, :], in_=cc_d[:, 0:256]), w=["ccf"])
        P.dve(lambda e: e.memset(onesrow[:, :], 1.0), w=["ones"])
        P.dve(lambda e: e.memset(scr[:, :], 0.0), w=["scr_a", "scr_d", "scr_p"])
        P.dve(lambda e: e.memset(small[:, :], 0.0), w=["small"])
        P.dve(lambda e: e.memset(epsc, EPS), r=["small"], w=["eps"])
        P.barrier()

        precast = []
        for l_ in range(L):
            wv_ = win_d[l_].rearrange("(k p) c -> p k c", p=128)
            ov_ = wout_d[l_].rearrange("(k p) c -> p k c", p=128)
            c_lo = 2056 if l_ == 0 else 0
            for k in range(8):
                precast.append((wbf_d[l_][:, k, c_lo:DIN], wv_[:, k, c_lo:DIN]))
            for k in range(0, 16, 4):
                precast.append((wobf_d[l_][:, k:k + 4, :], ov_[:, k:k + 4, :]))

        def do_layer(l):
            src_d = x_d if l == 0 else hmid_d
            dst_d = out_d if l == L - 1 else hmid_d
            winv = win_d[l].rearrange("(k p) c -> p k c", p=128)
            woutv = wout_d[l].rearrange("(k p) c -> p k c", p=128)
            P.dma("sp", lambda e, l=l: e.dma_start(out=ppt[:, :], in_=pp_d[l]), w=["ppt"])
            P.dma("pool", lambda e, l=l: e.dma_start(out=lwb[:, :, :], in_=lw_d[l]), w=["lwb"])
            P.dve(lambda e: e.tensor_scalar(out=small[:, 0:1], in0=ppt[:, PP_GQ:PP_GQ + 1], scalar1=0.125,
                                            scalar2=None, op0=ALU.mult), r=["ppt"], w=["small"])
            P.dve(lambda e: e.tensor_copy(out=small[:, 1:2], in_=ppt[:, PP_GK:PP_GK + 1]), r=["ppt"], w=["small"])
            P.dve(lambda e: e.tensor_scalar(out=small[0:8, 2:3], in0=ppt[0:8, PP_FB:PP_FB + 1], scalar1=-1.0,
                                            scalar2=None, op0=ALU.mult), r=["ppt"], w=["small"])
            P.act(lambda e: e.activation(out=small[0:16, 3:4], in_=ppt[0:16, PP_ALOG:PP_ALOG + 1], func=AF.Exp),
                  r=["ppt"], w=["small3"])
            P.dve(lambda e: e.tensor_scalar(out=small[0:16, 3:4], in0=small[0:16, 3:4], scalar1=-1.0,
                                            scalar2=None, op0=ALU.mult), r=["small3"], w=["small3"])
            P.act(lambda e: e.activation(out=small[:, 4:8], in_=ppt[:, PP_LAM:PP_LAM + 4], func=AF.Exp, scale=-1.0),
                  r=["ppt"], w=["small4"])
            P.act(lambda e: e.activation(out=small[:, 4:8], in_=small[:, 4:8], func=AF.Ln, bias=1.0),
                  r=["small4"], w=["small4"])
            P.dve(lambda e: e.tensor_scalar(out=small[:, 8:12], in0=small[:, 4:8], scalar1=-16.0,
                                            scalar2=None, op0=ALU.mult), r=["small4"], w=["small8"])
            P.dve(lambda e: e.tensor_scalar(out=small[:, 4:8], in0=small[:, 4:8], scalar1=-8.0,
                                            scalar2=None, op0=ALU.mult), r=["small4", "small8"], w=["small4"])
            P.dve(lambda e: e.tensor_scalar(out=small[:, 12:16], in0=small[:, 4:8], scalar1=0.5,
                                            scalar2=None, op0=ALU.mult), r=["small4"], w=["small12"])
            P.dve(lambda e: e.tensor_scalar(out=small[:, 44:48], in0=ppt[:, PP_BA:PP_BA + 4], scalar1=0.5,
                                            scalar2=None, op0=ALU.mult), r=["ppt"], w=["small44"])
            P.dve(lambda e: e.tensor_scalar(out=small[:, 48:52], in0=ppt[:, PP_BX:PP_BX + 4], scalar1=0.5,
                                            scalar2=None, op0=ALU.mult), r=["ppt"], w=["small48"])
            gqs = small[:, 0:1]
            gks = small[:, 1:2]
            nfb = small[0:8, 2:3]
            Acol = small[0:16, 3:4]
            P.barrier()

            with contextlib.ExitStack() as sa:
                sb = mk(sa, "_l%d" % l)
                arena = sb("arenaA", [128, 49152], BF16)
                Kc = arena[:, 0:16384].rearrange("p (a t) -> p a t", a=4)
                Vc = arena[:, 16384:49152].rearrange("p (j h c) -> p j h c", j=32, h=8)
                WA = sb("WA", [128, 8, 2056], BF16)
                hxa = sb("hxa", [128, 2, 1024], F32)
                junk = sb("junkA", [128, 1024], BF16)
                un = [sb("unA%d" % i, [128, 1024], BF16) for i in range(2)]
                uT = sb("uTA", [128, 8, 512], BF16)
                qz = sb("qz", [128, 8, 512], BF16)
                zs = sb("zs", [128, 4, 512], BF16)
                sq = [sb("sq%d" % i, [128, 512], BF16) for i in range(2)]
                sd = [sb("sd%d" % i, [128, 512], F32) for i in range(2)]
                qraw = [sb("qraw%d" % i, [128, 512], BF16) for i in range(2)]
                lfn = sb("lfn", [8, 512], F32)
                cn = [sb("cn%d" % i, [8, 512], F32) for i in range(2)]
                chat = sb("chat", [128, 512], BF16)
                cbc = sb("cbc", [8, 128], F32)
                cn_tok = sb("cn_tok", [128, 32, 8], F32)
                cref = sb("cref", [128, 8], F32)
                biasT = sb("biasT", [128, 32, 8], F32)
                pT = [sb("pT%d" % i, [128, 512], BF16) for i in range(4)]
                rl = sb("rl", [128, 512], F32)
                yat = sb("yat", [128, 4, 512], BF16)

                for k in range(8):
                    if l == 0:
                        P.dma("pool", lambda e, k=k: e.dma_start(out=WA[:, k, :], in_=winv[:, k, 0:2056]), w=["WA%d" % k])
                    else:
                        P.dma("sp", lambda e, k=k: e.dma_start(out=WA[:, k, :], in_=wbf_d[l][:, k, 0:2056]), w=["WA%d" % k])
                WAt = ["WA%d" % k for k in range(8)]
                gbc = ppt[:, PP_G:PP_G + 8].unsqueeze(2).to_broadcast([128, 8, 128])
                P.pool(lambda e: e.memset(qz[:, :, :], 0.0), w=["qn"])
                P.pool(lambda e: e.memset(chat[:, :], 0.0), w=["chat"])
                for h in range(8):
                    lo1 = 64 if h % 2 == 0 else 0
                    P.pool(lambda e, h=h, lo1=lo1: e.memset(Vc[:, :, h, lo1:lo1 + 64], 1.0), w=["Vc"])
                it = 0
                for T in range(nchunks):
                    t0 = 512 * T
                    nb = 4 * T + 4
                    for i in range(4):
                        r0 = t0 + 128 * i
                        hb = i % 2
                        P.dma("sp", lambda e, r0=r0, hb=hb: e.dma_start(out=hxa[:, hb, :], in_=src_d[r0:r0 + 128, :]),
                              w=["hxa%d" % hb])
                        P.act(lambda e, i=i, hb=hb: e.activation(out=junk[:, :], in_=hxa[:, hb, :], func=AF.Square,
                                                                 accum_out=small[:, 16 + i:17 + i]),
                              r=["hxa%d" % hb], w=["junk", "ss%d" % i])
                        P.act(lambda e, i=i: e.activation(out=small[:, 20 + i:21 + i], in_=small[:, 16 + i:17 + i], func=AF.Ln,
                                                          scale=1.0 / D, bias=epsc), r=["ss%d" % i], w=["rs0%d" % i])
                        P.act(lambda e, i=i: e.activation(out=small[:, 24 + i:25 + i], in_=small[:, 20 + i:21 + i], func=AF.Exp,
                                                          scale=-0.5), r=["rs0%d" % i], w=["rs%d" % i])
                        u = un[i % 2]
                        P.dve(lambda e, i=i, u=u, hb=hb: e.tensor_scalar(out=u[:, :], in0=hxa[:, hb, :],
                                                                         scalar1=small[:, 24 + i:25 + i], scalar2=None,
                                                                         op0=ALU.mult), r=["hxa%d" % hb, "rs%d" % i], w=[u.name])

                        def tr(e, u=u):
                            ins = None
                            for k in range(8):
                                ins = e.transpose(out=ptr[:, k, :], in_=u[:, 128 * k:128 * k + 128], identity=identb)
                            return ins
                        P.pe(tr, r=[u.name], w=["ptr"])
                        P.dve(lambda e, i=i: e.tensor_tensor(out=uT[:, :, 128 * i:128 * i + 128], in0=ptr[:, :, :], in1=gbc, op=ALU.mult),
                              r=["ptr"], w=["uT"])
                    P.dma("sp", lambda e, t0=t0: e.dma_start(out=utd_d[:, :, t0:t0 + 512], in_=uT[:, :, :]),
                          r=["uT"], w=["utd"])
                    if l == 0:
                        per = (len(precast) + nchunks - 1) // nchunks
                        for (o_, i_) in precast[per * T:per * T + per]:
                            P.dma("pool", lambda e, o_=o_, i_=i_: e.dma_start(out=o_, in_=i_), w=["wbf%d" % id(o_)])

                    def proj_fm(pt, c0, m):
                        def f(e):
                            ins = None
                            for k in range(8):
                                ins = e.matmul(pt[0:m, :], lhsT=WA[:, k, c0:c0 + m], rhs=uT[:, k, :],
                                               start=(k == 0), stop=(k == 7))
                            return ins
                        return f

                    P.pe(proj_fm(pm, 2048, 8), r=WAt + ["uT"], w=["pm"])
                    P.act(lambda e: e.activation(out=lfn[:, :], in_=pm[0:8, :], func=AF.Exp, scale=-1.0, bias=nfb),
                          r=["pm"], w=["lfn"])
                    P.act(lambda e: e.activation(out=lfn[:, :], in_=lfn[:, :], func=AF.Ln, bias=1.0),
                          r=["lfn"], w=["lfn"])
                    cnc = cn[T % 2]
                    cnp = cn[(T + 1) % 2]
                    P.dve(lambda e, T=T, cnc=cnc, cnp=cnp: e.tensor_tensor_scan(
                        out=cnc[:, :], data0=onesrow[0:8, :], data1=lfn[:, :],
                        initial=(0.0 if T == 0 else cnp[:, 511:512]), op0=ALU.mult, op1=ALU.add),
                        r=["lfn", cnp.name], w=[cnc.name])
                    P.dve(lambda e, cnc=cnc: e.tensor_scalar(out=chat[0:8, :], in0=cnc[:, :], scalar1=-1.0,
                                                             scalar2=cnc[:, 0:1], op0=ALU.mult, op1=ALU.add),
                          r=[cnc.name], w=["chat"])
                    P.dve(lambda e, cnc=cnc: e.tensor_copy(out=cbc[:, :], in_=cnc[:, 0:1].to_broadcast([8, 128])),
                          r=[cnc.name], w=["cbc"])

                    def trc(e, cnc=cnc):
                        for i in range(4):
                            e.transpose(out=pm[:, 32 + 8 * i:40 + 8 * i], in_=cnc[:, 128 * i:128 * i + 128],
                                        identity=ident[0:8, 0:8])
                        return e.transpose(out=pm[:, 64:72], in_=cbc[:, :], identity=ident[0:8, 0:8])
                    P.pe(trc, r=[cnc.name, "cbc"], w=["pm"])
                    P.dve(lambda e, T=T: e.tensor_copy(out=cn_tok[:, 4 * T:4 * T + 4, :],
                                                       in_=pm[:, 32:64].rearrange("p (a b) -> p a b", b=8)),
                          r=["pm"], w=["cn_tok"])
                    P.dve(lambda e: e.tensor_copy(out=cref[:, :], in_=pm[:, 64:72]), r=["pm"], w=["cref"])
                    P.dve(lambda e, nb=nb: e.tensor_tensor(out=biasT[:, 0:nb, :], in0=cn_tok[:, 0:nb, :],
                                                           in1=cref[:, :].unsqueeze(1).to_broadcast([128, nb, 8]),
                                                           op=ALU.subtract), r=["cn_tok", "cref"], w=["biasT"])

                    def qk_group(c0, gcol, dst, dtoks, idx, split=None):
                        pt = pj[idx % 2]
                        s_ = sq[idx % 2]
                        d_ = sd[idx % 2]
                        qr_ = qraw[idx % 2]
                        P.pe(proj_fm(pt, c0, 128), r=WAt + ["uT"], w=[pt.name])
                        P.act(lambda e: e.activation(out=s_[:, :], in_=pt[:, :], func=AF.Square), r=[pt.name], w=[s_.name])
                        P.dve(lambda e: e.tensor_copy(out=qr_[:, :], in_=pt[:, :]), r=[pt.name, s_.name], w=[qr_.name])
                        P.pe(lambda e: e.matmul(ptrfA[:, :], lhsT=BDb, rhs=s_[:, :], start=True, stop=True),
                             r=[s_.name], w=["ptr"])
                        P.act(lambda e: e.activation(out=d_[:, :], in_=ptrfA[:, :], func=AF.Ln, bias=epsc), r=["ptr"], w=[d_.name])
                        P.act(lambda e: e.activation(out=d_[:, :], in_=d_[:, :], func=AF.Exp, scale=-0.5), r=[d_.name], w=[d_.name])
                        if split is None:
                            P.dve(lambda e: e.scalar_tensor_tensor(out=dst, in0=qr_[:, :], scalar=gcol, in1=d_[:, :],
                                                                   op0=ALU.mult, op1=ALU.mult),
                                  r=[qr_.name, d_.name], w=dtoks)
                        else:
                            for hh in range(2):
                                rs_ = slice(64 * hh, 64 * hh + 64)
                                P.dve(lambda e, rs_=rs_, hh=hh: e.scalar_tensor_tensor(
                                    out=qz[rs_, 2 * split + hh, :], in0=qr_[rs_, :], scalar=gcol[rs_, :], in1=d_[rs_, :],
                                    op0=ALU.mult, op1=ALU.mult), r=[qr_.name, d_.name], w=dtoks)
                    gi = 0
                    for p_ in range(4):
                        qk_group(512 + 128 * p_, gks, Kc[:, p_, t0:t0 + 512], ["KV"], gi)
                        gi += 1
                    for p_ in range(4):
                        qk_group(128 * p_, gqs, None, ["qn"], gi, split=p_)
                        gi += 1
                    for p_ in range(4):
                        pt = pj[gi % 2]
                        gi += 1
                        P.pe(proj_fm(pt, 1536 + 128 * p_, 128), r=WAt + ["uT"], w=[pt.name])
                        P.act(lambda e, pt=pt, p_=p_: e.activation(out=zs[:, p_, :], in_=pt[:, :], func=AF.Silu),
                              r=[pt.name], w=["zs"])
                    for i in range(4):
                        pt = pj[gi % 2]
                        gi += 1

                        def vproj(e, pt=pt, i=i):
                            ins = None
                            for k in range(8):
                                ins = e.matmul(pt[:, :], lhsT=uT[:, k, 128 * i:128 * i + 128], rhs=WA[:, k, 1024:1536],
                                               start=(k == 0), stop=(k == 7))
                            return ins
                        P.pe(vproj, r=WAt + ["uT"], w=[pt.name])
                        j = 4 * T + i
                        ptv = pt[:, :].rearrange("p (a two c) -> p a two c", two=2, c=64)
                        vcv = Vc[:, j, :, :].rearrange("p (a two) c -> p a two c", two=2)
                        P.dve(lambda e, ptv=ptv, vcv=vcv: e.tensor_copy(out=vcv[:, :, 0, 0:64], in_=ptv[:, :, 0, :]),
                              r=[pt.name], w=["KV"])
                        P.dve(lambda e, ptv=ptv, vcv=vcv: e.tensor_copy(out=vcv[:, :, 1, 64:128], in_=ptv[:, :, 1, :]),
                              r=[pt.name], w=["KV"])

                    if T == 0:
                        jorder = [0, 1, 2, 3]
                    else:
                        jorder = [4 * T, 4 * T + 1, 4 * T + 2, 4 * T + 3] + list(range(4 * T))
                    for h in range(8):
                        p_ = h // 2
                        lo = 64 * (h % 2)
                        po = (pa[3], pm)[h % 2]
                        for n_, j in enumerate(jorder):
                            c0 = 0 if j < 4 * T else 128 * (j - 4 * T)
                            pst_ = pa[it % 3]
                            pt_ = pT[it % 4]
                            it += 1

                            def f(e, j=j, c0=c0, pst_=pst_, h=h, p_=p_):
                                e.matmul(pst_[:, c0:512], lhsT=Kc[:, p_, 128 * j:128 * j + 128],
                                         rhs=qz[:, h, c0:512], start=True, stop=False)
                                return e.matmul(pst_[:, c0:512], lhsT=selH[:, h, :], rhs=chat[:, c0:512],
                                                start=False, stop=True)
                            P.pe(f, r=["KV", "Vc", "qn", "chat"], w=[pst_.name])
                            P.act(lambda e, j=j, c0=c0, pst_=pst_, pt_=pt_, h=h: e.activation(
                                out=pt_[:, c0:512], in_=pst_[:, c0:512], func=AF.Exp, bias=biasT[:, j, h:h + 1]),
                                r=[pst_.name, "biasT"], w=[pt_.name])
                            if j >= 4 * T:
                                P.pool(lambda e, c0=c0, pt_=pt_: e.tensor_tensor(out=pt_[:, c0:c0 + 128], in0=pt_[:, c0:c0 + 128],
                                                                                 in1=trib, op=ALU.mult),
                                       r=[pt_.name], w=[pt_.name])
                            first = (n_ == 0)
                            last = (n_ == len(jorder) - 1) and T > 0
                            P.pe(lambda e, j=j, c0=c0, pt_=pt_, h=h, po=po, first=first, last=last: e.matmul(
                                po[:, c0:512], lhsT=Vc[:, j, h, :], rhs=pt_[:, c0:512], start=first, stop=last),
                                r=["KV", "Vc", pt_.name], w=[po.name])
                        if T == 0:
                            P.pe(lambda e, h=h, po=po: e.matmul(po[:, :], lhsT=zero128, rhs=qz[:, h, :], start=False, stop=True),
                                 r=["qn"], w=[po.name])
                        ol = po[lo:lo + 64, :]
                        ll = po[64 - lo:128 - lo, :]
                        P.dve(lambda e, ll=ll, lo=lo: e.reciprocal(out=rl[lo:lo + 64, :], in_=ll), r=[po.name], w=["rl"])
                        P.dve(lambda e, lo=lo, p_=p_: e.tensor_tensor(out=rl[lo:lo + 64, :], in0=rl[lo:lo + 64, :],
                                                                     in1=zs[lo:lo + 64, p_, :], op=ALU.mult),
                              r=["rl", "zs"], w=["rl"])
                        P.dve(lambda e, ol=ol, lo=lo, p_=p_: e.tensor_tensor(out=yat[lo:lo + 64, p_, :], in0=ol,
                                                                            in1=rl[lo:lo + 64, :], op=ALU.mult),
                              r=[po.name, "rl"], w=["yat"])
                    P.dma("sp", lambda e, t0=t0: e.dma_start(out=yatd_d[:, :, t0:t0 + 512], in_=yat[:, :, :]),
                          r=["yat"], w=["yatd"])
                    if l == 0 and T == 0:
                        dump("uT", uT[:, :, :], [128, 8, 512], BF16, ["uT"])
                        dump("qz", qz[:, :, :], [128, 8, 512], BF16, ["qn"])
                        dump("cn", cn[0][:, :], [8, 512], F32, [cn[0].name])
                        dump("yat", yat[:, :, :], [128, 4, 512], BF16, ["yat"])
                    if l == 0 and T == 1:
                        dump("yat1", yat[:, :, :], [128, 4, 512], BF16, ["yat"])
                P.barrier()

            with contextlib.ExitStack() as sB:
                sb = mk(sB, "_l%d" % l)
                WB = sb("WB", [128, 8, 3600], BF16)
                uTb = sb("uTB", [128, 8, 512], BF16)
                junkb = sb("junkB", [128, 512], BF16)
                rawb = [sb("rawb%d" % i, [128, 515], BF16) for i in range(3)]
                hist = sb("hist", [128, 16, 3], BF16)
                dg = sb("dg", [128, 64, 128], BF16)
                xc = sb("xc", [128, 12, 512], BF16)
                xcl = sb("xcl", [128, 512], F32)
                xclb = sb("xclb", [128, 512], BF16)
                zl = sb("zl", [128, 4, 512], BF16)
                zsd = sb("zsd", [128, 4, 1024], BF16)
                dtt = sb("dtt", [16, 512], F32)
                acs = sb("acs", [16, 512], F32)
                lastb = sb("lastb", [16, 512], F32)
                r1 = sb("r1", [16, 512], F32)
                hib = sb("hib", [16, 512], BF16)
                D2 = sb("D2", [128, 512], BF16)
                nD2 = sb("nD2", [128, 512], BF16)
                tok3 = sb("tok3", [128, 12, 16], F32)
                ea_tok = sb("ea_tok", [128, 4, 16], F32)
                eal_tok = sb("eal_tok", [128, 4, 16], F32)
                wst = sb("wst", [128, 4, 16], F32)
                xs_tok = sb("xs_tok", [128, 1024], BF16)
                b_tok = sb("b_tok", [128, 2, 128], BF16)
                cb = sb("cb", [128, 2, 128], BF16)
                Lt = [sb("Lt%d" % i, [128, 512], BF16) for i in range(2)]
                Mt2 = [sb("Mt%d" % i, [128, 16, 128], BF16) for i in range(2)]
                xdt2 = [sb("xdt%d" % i, [128, 1024], BF16) for i in range(2)]
                xdd2 = [sb("xdd%d" % i, [128, 1024], BF16) for i in range(2)]
                xDs2 = [sb("xDs%d" % i, [128, 1024], BF16) for i in range(2)]
                prev = sb("prev", [128, 2, 512], F32)
                prevb = sb("prevb", [128, 2, 512], BF16)
                ysum2 = [sb("ysum%d" % i, [128, 512], F32) for i in range(2)]
                gated2 = [sb("gated%d" % i, [128, 512], F32) for i in range(2)]
                ysn2 = [sb("ysn%d" % i, [128, 512], BF16) for i in range(2)]
                mixT = sb("mixT", [128, 12, 512], BF16)
                lr = sb("lr", [128, 512], F32)
                li = sb("li", [128, 512], F32)
                la = sb("la", [128, 512], F32)
                lm = sb("lm", [128, 512], F32)
                hsb = [sb("hsb%d" % i, [128, 512], F32) for i in range(2)]
                hcar = sb("hcar", [128, 4], F32)

                for k in range(8):
                    P.dma("sp", lambda e, k=k: e.dma_start(out=WB[:, k, :], in_=wbf_d[l][:, k, 2056:DIN]), w=["WB%d" % k])
                WBt = ["WB%d" % k for k in range(8)]
                P.dve(lambda e: e.memset(hist[:, :, :], 0.0), w=["hist%d" % g for g in range(16)])
                P.dve(lambda e: e.memset(prev[:, :, :], 0.0), w=["prev0", "prev1"])
                P.dve(lambda e: e.memset(prevb[:, :, :], 0.0), w=["prevb0", "prevb1"])
                P.dve(lambda e: e.memset(D2[:, :], 0.0), w=["D2"])
                P.dve(lambda e: e.memset(nD2[:, :], 0.0), w=["nD2"])
                P.dve(lambda e: e.memset(hcar[:, :], 0.0), w=["hcar"])
                scw = ppt[:, PP_SCW:PP_SCW + 48].rearrange("p (g k) -> p g k", k=4)
                lcw = ppt[:, PP_LCW:PP_LCW + 16].rearrange("p (g k) -> p g k", k=4)
                for g in range(16):
                    wv_ = scw[:, g, :] if g < 12 else lcw[:, g - 12, :]
                    for k in range(4):
                        eng_ = P.pool if (4 * g + k) % 2 == 0 else P.dve
                        eng_(lambda e, g=g, k=k, wv_=wv_: e.tensor_scalar(out=dg[:, 4 * g + k, :], in0=identb, scalar1=wv_[:, k:k + 1],
                                                                         scalar2=None, op0=ALU.mult), w=["dg%d" % (4 * g + k)])
                Dbc = ppt[:, PP_DBC:PP_DBC + 16]
                gnb = ppt[:, PP_GN:PP_GN + 1024]
                pz, pyd, pyo, pst = pa
                pring = [pj[0], pj[1], pa[1], pa[2]]
                pmb = pm[:, 0:256].bitcast(BF16).rearrange("p (q c) -> p q c", q=4)
                ptrf = ptr[:, :, :].rearrange("p a b -> p (a b)").bitcast(F32)
                hcount = 0
                for T in range(nchunks):
                    t0 = 512 * T
                    P.dma("sp", lambda e, t0=t0: e.dma_start(out=uTb[:, :, :], in_=utd_d[:, :, t0:t0 + 512]), w=["uT"])

                    def proj_b(pt, c0, m):
                        def f(e):
                            ins = None
                            for k in range(8):
                                ins = e.matmul(pt[0:m, :], lhsT=WB[:, k, c0:c0 + m], rhs=uTb[:, k, :],
                                               start=(k == 0), stop=(k == 7))
                            return ins
                        return f
                    gi = 0
                    P.pe(proj_b(pm, 2560, 16), r=WBt + ["uT"], w=["pm"])
                    P.act(lambda e: e.activation(out=dtt[:, :], in_=pm[0:16, :], func=AF.Exp,
                                                 bias=ppt[0:16, PP_DTB:PP_DTB + 1]), r=["pm"], w=["dtt"])
                    P.act(lambda e: e.activation(out=dtt[:, :], in_=dtt[:, :], func=AF.Ln, bias=1.0), r=["dtt"], w=["dtt"])
                    P.dve(lambda e: e.tensor_scalar(out=r1[:, :], in0=dtt[:, :], scalar1=Acol, scalar2=None, op0=ALU.mult),
                          r=["dtt"], w=["r1"])
                    for c in range(4):
                        P.dve(lambda e, c=c: e.tensor_tensor_scan(
                            out=acs[:, 128 * c:128 * c + 128], data0=onesrow[0:16, 0:128], data1=r1[:, 128 * c:128 * c + 128],
                            initial=0.0, op0=ALU.mult, op1=ALU.add), r=["r1"], w=["acs"])
                    P.dve(lambda e: e.tensor_copy(
                        out=lastb[:, :].rearrange("p (c l) -> p c l", c=4),
                        in_=acs[:, :].rearrange("p (c l) -> p c l", c=4)[:, :, 127:128].to_broadcast([16, 4, 128])),
                        r=["acs"], w=["lastb"])

                    def trd(e):
                        ins = None
                        for c in range(4):
                            e.transpose(out=pm[:, 16 * c:16 * c + 16], in_=dtt[:, 128 * c:128 * c + 128], identity=ident[0:16, 0:16])
                            e.transpose(out=pm[:, 64 + 16 * c:80 + 16 * c], in_=acs[:, 128 * c:128 * c + 128], identity=ident[0:16, 0:16])
                            ins = e.transpose(out=pm[:, 128 + 16 * c:144 + 16 * c], in_=lastb[:, 128 * c:128 * c + 128],
                                              identity=ident[0:16, 0:16])
                        return ins
                    P.pe(trd, r=["dtt", "acs", "lastb"], w=["pm"])
                    P.dve(lambda e: e.tensor_copy(out=tok3[:, :, :], in_=pm[:, 0:192].rearrange("p (a b) -> p a b", b=16)),
                          r=["pm"], w=["tok3"])
                    P.act(lambda e: e.activation(out=ea_tok[:, :, :], in_=tok3[:, 4:8, :], func=AF.Exp), r=["tok3"], w=["ea_tok"])
                    P.act(lambda e: e.activation(out=eal_tok[:, :, :], in_=tok3[:, 8:12, :], func=AF.Exp), r=["tok3"], w=["eal_tok"])
                    P.dve(lambda e: e.tensor_tensor(out=wst[:, :, :], in0=tok3[:, 8:12, :], in1=tok3[:, 4:8, :], op=ALU.subtract),
                          r=["tok3"], w=["wst"])
                    P.act(lambda e: e.activation(out=wst[:, :, :], in_=wst[:, :, :], func=AF.Exp), r=["wst"], w=["wst"])
                    P.dve(lambda e: e.tensor_tensor(out=wst[:, :, :], in0=wst[:, :, :], in1=tok3[:, 0:4, :], op=ALU.mult),
                          r=["wst", "tok3"], w=["wst"])
                    P.dve(lambda e: e.tensor_copy(out=hib[:, :], in_=acs[:, :]), r=["acs"], w=["hib"])
                    P.dve(lambda e: e.tensor_tensor(out=r1[:, :], in0=acs[:, :], in1=hib[:, :], op=ALU.subtract),
                          r=["acs", "hib"], w=["r1"])
                    P.dve(lambda e: e.tensor_copy(out=D2[0:16, :], in_=hib[:, :]), r=["hib"], w=["D2"])
                    P.dve(lambda e: e.tensor_copy(out=D2[32:48, :], in_=r1[:, :]), r=["r1"], w=["D2"])
                    P.dve(lambda e: e.tensor_scalar(out=nD2[0:48, :], in0=D2[0:48, :], scalar1=-1.0, scalar2=None, op0=ALU.mult),
                          r=["D2"], w=["nD2"])
                    for m in range(4):
                        pt = pring[gi % 4]
                        gi += 1
                        P.pe(proj_b(pt, 3088 + 128 * m, 128), r=WBt + ["uT"], w=[pt.name])
                        P.act(lambda e, pt=pt, m=m: e.activation(out=zl[:, m, :], in_=pt[:, :], func=AF.Silu),
                              r=[pt.name], w=["zl"])
                    for g in range(16):
                        pt = pring[gi % 4]
                        gi += 1
                        c0 = 1024 + 128 * g if g < 12 else 2576 + 128 * (g - 12)
                        rb = rawb[g % 3]
                        P.pe(proj_b(pt, c0, 128), r=WBt + ["uT"], w=[pt.name])
                        P.dve(lambda e, g=g, rb=rb: e.tensor_copy(out=rb[:, 0:3], in_=hist[:, g, :]), r=["hist%d" % g], w=[rb.name])
                        P.act(lambda e, pt=pt, rb=rb: e.activation(out=rb[:, 3:515], in_=pt[:, :], func=AF.Copy),
                              r=[pt.name], w=[rb.name])
                        P.dve(lambda e, g=g, rb=rb: e.tensor_copy(out=hist[:, g, :], in_=rb[:, 512:515]), r=[rb.name], w=["hist%d" % g])
                        pc = pring[gi % 4]
                        gi += 1

                        def convmm(e, g=g, rb=rb, pc=pc):
                            ins = None
                            for k in range(4):
                                ins = e.matmul(pc[:, :], lhsT=dg[:, 4 * g + k, :], rhs=rb[:, k:k + 512], start=(k == 0), stop=(k == 3))
                            return ins
                        P.pe(convmm, r=[rb.name] + ["dg%d" % (4 * g + k) for k in range(4)], w=[pc.name])
                        if g < 12:
                            P.act(lambda e, g=g, pc=pc: e.activation(out=xc[:, g, :], in_=pc[:, :], func=AF.Silu,
                                                                     bias=ppt[:, PP_SCB + g:PP_SCB + g + 1]), r=[pc.name], w=["xc"])
                            continue
                        P.act(lambda e, g=g, pc=pc: e.activation(out=xcl[:, :], in_=pc[:, :], func=AF.Identity,
                                                                 bias=ppt[:, PP_LCB + g - 12:PP_LCB + g - 11]), r=[pc.name], w=["xcl"])
                        m = g - 12
                        P.act(lambda e: e.activation(out=xclb[:, :], in_=xcl[:, :], func=AF.Copy), r=["xcl"], w=["xclb"])
                        pga = pring[gi % 4]
                        pgx = pring[(gi + 1) % 4]
                        gi += 2
                        P.pe(lambda e, m=m, pga=pga: e.matmul(pga[:, :], lhsT=lwb[:, m, :], rhs=xclb[:, :], start=True, stop=True),
                             r=["xclb"], w=[pga.name])
                        P.pe(lambda e, m=m, pgx=pgx: e.matmul(pgx[:, :], lhsT=lwb[:, 4 + m, :], rhs=xclb[:, :], start=True, stop=True),
                             r=["xclb"], w=[pgx.name])
                        P.act(lambda e, m=m, pga=pga: e.activation(out=lr[:, :], in_=pga[:, :], func=AF.Tanh, scale=0.5,
                                                                   bias=small[:, 44 + m:45 + m]), r=[pga.name], w=["lr"])
                        P.act(lambda e, m=m, pgx=pgx: e.activation(out=li[:, :], in_=pgx[:, :], func=AF.Tanh, scale=0.5,
                                                                   bias=small[:, 48 + m:49 + m]), r=[pgx.name], w=["li"])
                        P.act(lambda e, m=m: e.activation(out=la[:, :], in_=lr[:, :], func=AF.Exp, scale=small[:, 12 + m:13 + m],
                                                          bias=small[:, 12 + m:13 + m]), r=["lr"], w=["la"])
                        P.act(lambda e, m=m: e.activation(out=lm[:, :], in_=lr[:, :], func=AF.Exp, scale=small[:, 4 + m:5 + m],
                                                          bias=small[:, 4 + m:5 + m]), r=["lr"], w=["lm"])
                        P.dve(lambda e: e.tensor_scalar(out=lm[:, :], in0=lm[:, :], scalar1=0.9999999, scalar2=-1.0, op0=ALU.min, op1=ALU.mult),
                              r=["lm"], w=["lm"])
                        P.act(lambda e: e.activation(out=lm[:, :], in_=lm[:, :], func=AF.Ln, bias=1.0), r=["lm"], w=["lm"])
                        P.act(lambda e: e.activation(out=lm[:, :], in_=lm[:, :], func=AF.Exp, scale=0.5), r=["lm"], w=["lm"])
                        if T == 0:
                            P.dve(lambda e: e.memset(lm[:, 0:1], 1.0), r=["lm"], w=["lm"])
                        P.dve(lambda e: e.scalar_tensor_tensor(out=li[:, :], in0=li[:, :], scalar=1.0, in1=xcl[:, :],
                                                               op0=ALU.add, op1=ALU.mult), r=["li", "xcl"], w=["li"])
                        P.dve(lambda e: e.scalar_tensor_tensor(out=li[:, :], in0=li[:, :], scalar=0.5, in1=lm[:, :],
                                                               op0=ALU.mult, op1=ALU.mult), r=["li", "lm"], w=["li"])
                        hc = hsb[hcount % 2]
                        hcount += 1
                        P.dve(lambda e, m=m, hc=hc: e.tensor_tensor_scan(
                            out=hc[:, :], data0=la[:, :], data1=li[:, :], initial=hcar[:, m:m + 1],
                            op0=ALU.mult, op1=ALU.add), r=["la", "li", "hcar"], w=[hc.name])
                        P.dve(lambda e, m=m, hc=hc: e.tensor_copy(out=hcar[:, m:m + 1], in_=hc[:, 511:512]), r=[hc.name], w=["hcar"])
                        P.dve(lambda e, m=m, hc=hc: e.tensor_tensor(out=mixT[:, 8 + m, :], in0=hc[:, :], in1=zl[:, m, :], op=ALU.mult),
                               r=[hc.name, "zl"], w=["mixT"])
                    for c in range(4):
                        for hf in range(2):
                            pt = pring[gi % 4]
                            gi += 1

                            def zproj(e, pt=pt, c=c, hf=hf):
                                ins = None
                                for k in range(8):
                                    ins = e.matmul(pt[:, :], lhsT=uTb[:, k, 128 * c:128 * c + 128],
                                                   rhs=WB[:, k, 512 * hf:512 * hf + 512], start=(k == 0), stop=(k == 7))
                                return ins
                            P.pe(zproj, r=WBt + ["uT"], w=[pt.name])
                            P.act(lambda e, pt=pt, hf=hf, c=c: e.activation(out=zsd[:, c, 512 * hf:512 * hf + 512], in_=pt[:, :],
                                                                            func=AF.Silu), r=[pt.name], w=["zsd%d" % c])
                    for c in range(4):
                        cs = slice(128 * c, 128 * c + 128)
                        Mt = Mt2[c % 2]
                        xdt = xdt2[c % 2]
                        xdd = xdd2[c % 2]
                        xDs = xDs2[c % 2]

                        def trx(e, cs=cs):
                            ins = None
                            for g in range(8):
                                ins = e.transpose(out=ptr[:, g, :], in_=xc[:, g, cs], identity=identb)
                            return ins
                        P.pe(trx, r=["xc"], w=["ptr"])
                        P.dve(lambda e: e.tensor_copy(out=xs_tok[:, :].rearrange("p (g c) -> p g c", g=8), in_=ptr[:, :, :]),
                              r=["ptr"], w=["xs_tok"])

                        def trb(e, cs=cs):
                            e.transpose(out=ptr[:, 0, :], in_=xc[:, 8, cs], identity=identb)
                            return e.transpose(out=ptr[:, 1, :], in_=xc[:, 9, cs], identity=identb)
                        P.pe(trb, r=["xc"], w=["ptr"])
                        P.dve(lambda e: e.tensor_copy(out=b_tok[:, :, :], in_=ptr[:, 0:2, :]), r=["ptr"], w=["b_tok"])

                        def cbm(e, cs=cs):
                            e.matmul(pm[:, 256:384], lhsT=xc[:, 8, cs], rhs=xc[:, 10, cs], start=True, stop=True)
                            return e.matmul(pm[:, 384:512], lhsT=xc[:, 9, cs], rhs=xc[:, 11, cs], start=True, stop=True)
                        P.pe(cbm, r=["xc"], w=["pm"])
                        P.act(lambda e: e.activation(out=cb[:, :, :], in_=pm[:, 256:512].rearrange("p (g c) -> p g c", g=2),
                                                     func=AF.Copy), r=["pm"], w=["cb"])
                        hs3 = xs_tok[:, :].rearrange("p (h c) -> p h c", h=16)
                        P.dve(lambda e, c=c, hs3=hs3, xdt=xdt: e.tensor_tensor(
                            out=xdt[:, :].rearrange("p (h c) -> p h c", h=16), in0=hs3,
                            in1=tok3[:, c, :].unsqueeze(2).to_broadcast([128, 16, 64]), op=ALU.mult),
                            r=["xs_tok", "tok3"], w=[xdt.name])
                        P.dve(lambda e, c=c, hs3=hs3, xdd=xdd: e.tensor_tensor(
                            out=xdd[:, :].rearrange("p (h c) -> p h c", h=16), in0=hs3,
                            in1=wst[:, c, :].unsqueeze(2).to_broadcast([128, 16, 64]), op=ALU.mult),
                            r=["xs_tok", "wst"], w=[xdd.name])
                        P.dve(lambda e, hs3=hs3, xDs=xDs: e.tensor_tensor(
                            out=xDs[:, :].rearrange("p (h c) -> p h c", h=16), in0=hs3,
                            in1=Dbc.unsqueeze(2).to_broadcast([128, 16, 64]), op=ALU.mult),
                            r=["xs_tok"], w=[xDs.name])
                        for hg in range(4):
                            ltile = Lt[hg % 2]

                            pz, pzt = ((pa[0], pa[0].name), (ptrf, "ptr"))[hg % 2]

                            def zmm(e, hg=hg, cs=cs, pz=pz):
                                for q_ in range(4):
                                    h = 4 * hg + q_
                                    e.matmul(pz[:, 128 * q_:128 * q_ + 128], lhsT=selR[:, h, :], rhs=D2[:, cs],
                                             start=(q_ == 0), stop=False)
                                    e.matmul(pz[:, 128 * q_:128 * q_ + 128], lhsT=nD2[:, cs], rhs=selR[:, h, :],
                                             start=False, stop=False)
                                return e.matmul(pz[:, :], lhsT=identb, rhs=mneg4, start=False, stop=True)
                            P.pe(zmm, r=["D2", "nD2"], w=[pzt])
                            P.act(lambda e, ltile=ltile, pz=pz: e.activation(out=ltile[:, :], in_=pz[:, :], func=AF.Exp),
                                  r=[pzt], w=[ltile.name])
                            g = hg // 2
                            P.dve(lambda e, hg=hg, g=g, ltile=ltile, Mt=Mt: e.tensor_tensor(
                                out=Mt[:, 4 * hg:4 * hg + 4, :], in0=ltile[:, :].rearrange("p (q c) -> p q c", q=4),
                                in1=cb[:, g, :].unsqueeze(1).to_broadcast([128, 4, 128]), op=ALU.mult),
                                r=[ltile.name, "cb"], w=[Mt.name])
                        for g in range(2):
                            gs = slice(512 * g, 512 * g + 512)
                            n2 = (2 * c + g) % 2
                            pyd = (pa[1], pj[0])[n2]
                            pyo = (pa[2], pj[1])[n2]
                            ysum = ysum2[n2]
                            gated = gated2[n2]
                            ysn = ysn2[n2]

                            def ydm(e, g=g, gs=gs, pyd=pyd, Mt=Mt, xdt=xdt, xDs=xDs):
                                for q_ in range(8):
                                    h = 8 * g + q_
                                    e.matmul(pyd[:, 64 * q_:64 * q_ + 64], lhsT=Mt[:, h, :], rhs=xdt[:, 64 * h:64 * h + 64],
                                             start=(q_ == 0), stop=False)
                                return e.matmul(pyd[:, :], lhsT=identb, rhs=xDs[:, gs], start=False, stop=True)
                            P.pe(ydm, r=[xDs.name, Mt.name, xdt.name], w=[pyd.name])
                            P.pe(lambda e, g=g, cs=cs, pyo=pyo: e.matmul(pyo[:, :], lhsT=xc[:, 10 + g, cs], rhs=prevb[:, g, :],
                                                                         start=True, stop=True), r=["xc", "prevb%d" % g], w=[pyo.name])
                            P.pe(lambda e, g=g, gs=gs, xdd=xdd: e.matmul(pst[:, :], lhsT=b_tok[:, g, :], rhs=xdd[:, gs],
                                                                         start=True, stop=True), r=["b_tok", xdd.name], w=[pst.name])
                            el_b = eal_tok[:, c, 8 * g:8 * g + 8].unsqueeze(2).to_broadcast([128, 8, 64])
                            P.dve(lambda e, g=g, el_b=el_b: e.tensor_tensor(
                                out=prev[:, g, :].rearrange("p (h c) -> p h c", h=8), in0=prev[:, g, :].rearrange("p (h c) -> p h c", h=8),
                                in1=el_b, op=ALU.mult), r=["prev%d" % g, "prevb%d" % g, "eal_tok"], w=["prev%d" % g])
                            P.dve(lambda e, g=g: e.tensor_tensor(out=prev[:, g, :], in0=pst[:, :], in1=prev[:, g, :], op=ALU.add),
                                  r=[pst.name, "prev%d" % g], w=["prev%d" % g])
                            P.act(lambda e, g=g: e.activation(out=prevb[:, g, :], in_=prev[:, g, :], func=AF.Copy),
                                  r=["prev%d" % g], w=["prevb%d" % g])
                            ea_b = ea_tok[:, c, 8 * g:8 * g + 8].unsqueeze(2).to_broadcast([128, 8, 64])
                            P.dve(lambda e, ea_b=ea_b, ysum=ysum, pyo=pyo: e.tensor_tensor(
                                out=ysum[:, :].rearrange("p (h c) -> p h c", h=8), in0=pyo[:, :].rearrange("p (h c) -> p h c", h=8),
                                in1=ea_b, op=ALU.mult), r=[pyo.name, "ea_tok"], w=[ysum.name])
                            P.dve(lambda e, ysum=ysum, pyd=pyd: e.tensor_tensor(out=ysum[:, :], in0=pyd[:, :], in1=ysum[:, :], op=ALU.add),
                                  r=[pyd.name, ysum.name], w=[ysum.name])
                            P.dve(lambda e, gs=gs, c=c, ysum=ysum, gated=gated: e.tensor_tensor(out=gated[:, :], in0=ysum[:, :], in1=zsd[:, c, gs], op=ALU.mult),
                                  r=[ysum.name, "zsd%d" % c], w=[gated.name])
                            q0 = 32 + 4 * n2
                            P.act(lambda e, gated=gated, q0=q0: e.activation(out=junkb[:, :], in_=gated[:, :], func=AF.Square,
                                                                             accum_out=small[:, q0:q0 + 1]), r=[gated.name], w=["junk", "ssq%d" % n2])
                            P.act(lambda e, q0=q0: e.activation(out=small[:, q0 + 1:q0 + 2], in_=small[:, q0:q0 + 1], func=AF.Ln,
                                                                scale=1.0 / 512, bias=epsc), r=["ssq%d" % n2], w=["ssq1%d" % n2])
                            P.act(lambda e, q0=q0: e.activation(out=small[:, q0 + 2:q0 + 3], in_=small[:, q0 + 1:q0 + 2], func=AF.Exp, scale=-0.5),
                                  r=["ssq1%d" % n2], w=["ssq2%d" % n2])
                            P.dve(lambda e, gs=gs, ysn=ysn, gated=gated, q0=q0: e.scalar_tensor_tensor(
                                out=ysn[:, :], in0=gated[:, :], scalar=small[:, q0 + 2:q0 + 3], in1=gnb[:, gs], op0=ALU.mult, op1=ALU.mult),
                                r=[gated.name, "ssq2%d" % n2], w=[ysn.name])

                            def try_(e, ysn=ysn):
                                ins = None
                                for q_ in range(4):
                                    ins = e.transpose(out=pmb[:, q_, :], in_=ysn[:, 128 * q_:128 * q_ + 128], identity=identb)
                                return ins
                            P.pe(try_, r=[ysn.name], w=["pm"])
                            P.dve(lambda e, g=g, cs=cs: e.tensor_copy(out=mixT[:, 4 * g:4 + 4 * g, cs], in_=pmb[:, 0:4, :]),
                                  r=["pm"], w=["mixT"])
                    P.dma("sp", lambda e, t0=t0: e.dma_start(out=ymd_d[:, :, t0:t0 + 512], in_=mixT[:, :, :]),
                          r=["mixT"], w=["ymd"])
                    if l == 0 and T == 0:
                        dump("mixT", mixT[:, :, :], [128, 12, 512], BF16, ["mixT"])
                    if l == 0 and T == 1:
                        dump("mixT1", mixT[:, :, :], [128, 12, 512], BF16, ["mixT"])
                P.barrier()

            with contextlib.ExitStack() as sC:
                sb = mk(sC, "_l%d" % l)
                WO = sb("WO", [128, 16, 1024], BF16)
                mixC = [sb("mixC%d" % i, [128, 16, 512], BF16) for i in range(2)]
                hxc = [sb("hxc%d" % i, [128, 4, 1024], F32) for i in range(2)]
                hout = [sb("hout%d" % i, [128, 1024], F32) for i in range(2)]
                for k in range(0, 16, 4):
                    P.dma("sp", lambda e, k=k: e.dma_start(out=WO[:, k:k + 4, :], in_=wobf_d[l][:, k:k + 4, :]), w=["WO"])
                gi = 0
                for T in range(nchunks):
                    t0 = 512 * T
                    mc = mixC[T % 2]
                    hx = hxc[T % 2]
                    P.dma("sp", lambda e, t0=t0, mc=mc: e.dma_start(out=mc[:, 0:4, :], in_=yatd_d[:, :, t0:t0 + 512]), w=[mc.name + "a"])
                    P.dma("sp", lambda e, t0=t0, mc=mc: e.dma_start(out=mc[:, 4:16, :], in_=ymd_d[:, :, t0:t0 + 512]), w=[mc.name + "b"])
                    P.dma("sp", lambda e, t0=t0, hx=hx: e.dma_start(
                        out=hx[:, :, :], in_=src_d[t0:t0 + 512, :].rearrange("(i p) d -> p i d", p=128)), w=[hx.name])
                    for i in range(4):
                        ho = hout[i % 2]
                        for hf in range(2):
                            pt = pj[gi % 2]
                            gi += 1

                            def oproj(e, pt=pt, i=i, hf=hf, mc=mc):
                                ins = None
                                for k in range(16):
                                    ins = e.matmul(pt[:, :], lhsT=mc[:, k, 128 * i:128 * i + 128],
                                                   rhs=WO[:, k, 512 * hf:512 * hf + 512], start=(k == 0), stop=(k == 15))
                                return ins
                            P.pe(oproj, r=["WO", mc.name + "a", mc.name + "b"], w=[pt.name])
                            P.dve(lambda e, pt=pt, i=i, hf=hf, ho=ho, hx=hx: e.tensor_tensor(
                                out=ho[:, 512 * hf:512 * hf + 512], in0=pt[:, :], in1=hx[:, i, 512 * hf:512 * hf + 512], op=ALU.add),
                                r=[pt.name, hx.name], w=[ho.name])
                        r0 = t0 + 128 * i
                        P.dma("sp", lambda e, ho=ho, r0=r0: e.dma_start(out=dst_d[r0:r0 + 128, :], in_=ho[:, :]),
                              r=[ho.name], w=["dst"])
                P.barrier()

        for l in range(L):
            do_layer(l)
        P.emit()
        stats = P.stats
    return nc, dbg_out, stats


_ML = None


def _consts():
    cc = np.zeros((128, NCC), np.float32)
    cc[:, CC_ID:CC_ID + 128] = np.eye(128, dtype=np.float32)
    s_ = np.arange(128)[:, None]
    t_ = np.arange(128)[None, :]
    cc[:, CC_TRI:CC_TRI + 128] = (s_ <= t_).astype(np.float32)
    cc[:, CC_MNEG:CC_MNEG + 512] = np.tile(np.where(s_ > t_, NEG, 0.0).astype(np.float32), (1, 4))
    bd = np.zeros((128, 128), np.float32)
    bd[0:64, 0:64] = 1.0 / 64
    bd[64:128, 64:128] = 1.0 / 64
    cc[:, CC_BD:CC_BD + 128] = bd
    for h in range(8):
        cc[h, CC_SELH + 128 * h:CC_SELH + 128 * h + 128] = 1.0
    for h in range(16):
        cc[h, CC_SELR + 128 * h:CC_SELR + 128 * h + 128] = 1.0
        cc[32 + h, CC_SELR + 128 * h:CC_SELR + 128 * h + 128] = 1.0
    return cc


def _pack(inputs, L):
    f = lambda a: np.asarray(a, np.float32)
    pp = np.zeros((L, 128, NPP), np.float32)
    lw = np.zeros((L, 128, 8, 128), np.float32)
    for l in range(L):
        pp[l, :, PP_G:PP_G + 8] = f(inputs["norm_g"])[l].reshape(8, 128).T
        pp[l, :, PP_GQ] = np.tile(f(inputs["q_norm_g"])[l], 2)
        pp[l, :, PP_GK] = np.tile(f(inputs["k_norm_g"])[l], 2)
        pp[l, 0:8, PP_FB] = f(inputs["forget_b"])[l]
        pp[l, 0:16, PP_DTB] = f(inputs["ssd_dt_bias"])[l]
        pp[l, 0:16, PP_ALOG] = f(inputs["ssd_a_log"])[l]
        pp[l, :, PP_BA:PP_BA + 4] = f(inputs["lru_b_a"])[l].reshape(4, 128).T
        pp[l, :, PP_BX:PP_BX + 4] = f(inputs["lru_b_x"])[l].reshape(4, 128).T
        pp[l, :, PP_LAM:PP_LAM + 4] = f(inputs["lru_lambda"])[l].reshape(4, 128).T
        scw = f(inputs["ssd_conv_w"])[l]
        pp[l, :, PP_SCW:PP_SCW + 48] = scw.T.reshape(12, 128, 4).transpose(1, 0, 2).reshape(128, 48)
        pp[l, :, PP_SCB:PP_SCB + 12] = f(inputs["ssd_conv_b"])[l].reshape(12, 128).T
        lcw = f(inputs["lru_conv_w"])[l]
        pp[l, :, PP_LCW:PP_LCW + 16] = lcw.T.reshape(4, 128, 4).transpose(1, 0, 2).reshape(128, 16)
        pp[l, :, PP_LCB:PP_LCB + 4] = f(inputs["lru_conv_b"])[l].reshape(4, 128).T
        pp[l, :, PP_DBC:PP_DBC + 16] = f(inputs["ssd_d"])[l][None, :]
        pp[l, :, PP_GN:PP_GN + 1024] = f(inputs["ssd_norm_g"])[l][None, :]
        for gate, nm in enumerate(("lru_w_a", "lru_w_x")):
            w = f(inputs[nm])[l]
            for n in range(8):
                m, half = n // 2, n % 2
                lw[l, 64 * half:64 * half + 64, 4 * gate + m, 64 * half:64 * half + 64] = w[n]
    return pp, lw


_CACHE = {}


def kernel(**inputs):
    L = 2
    x = np.ascontiguousarray(np.asarray(inputs["x"], np.float32))
    B = x.shape[0]
    if "nc" not in _CACHE:
        _CACHE["nc"] = build_program(L)[0]
    nc = _CACHE["nc"]
    pp, lw = _pack(inputs, L)
    cc = _consts()
    w_in = np.ascontiguousarray(np.asarray(inputs["w_in"], np.float32))
    w_out = np.ascontiguousarray(np.asarray(inputs["w_out"], np.float32))
    in_maps = [{"x": x[b], "w_in": w_in, "w_out": w_out, "pp": pp, "lw": lw, "cc": cc} for b in range(B)]
    res = run_bass_kernel_spmd(nc, in_maps, core_ids=list(range(B)))
    return np.stack([np.asarray(r["out"], np.float32) for r in res.results], axis=0)
```

```python
import contextlib
import numpy as np
import concourse.bass as bass
import concourse.mybir as mybir
from concourse.bass_utils import run_bass_kernel_spmd

F32 = mybir.dt.float32
BF16 = mybir.dt.bfloat16
ALU = mybir.AluOpType
AF = mybir.ActivationFunctionType

ENGS = ("pe", "act", "dve", "pool", "sp")

S = 4096
D = 1024
DIN = 5656
MIX = 2048
NT = 8
EPS = 1e-6
NEG = -30000.0

PP_G, PP_GQ, PP_GK, PP_FB, PP_DTB, PP_ALOG = 0, 8, 9, 10, 11, 12
PP_BA, PP_BX, PP_LAM = 13, 17, 21
PP_SCW, PP_SCB, PP_LCW, PP_LCB, PP_DBC, PP_GN = 25, 73, 85, 101, 105, 121
NPP = 1145
CC_ID, CC_TRI, CC_MNEG, CC_BD, CC_SELH, CC_SELR = 0, 128, 256, 768, 896, 1920
CC_ZERO = 3968
NCC = 4096


class _Op:
    __slots__ = ("eng", "fn", "reads", "writes", "deps", "is_dma", "signaled",
                 "sem", "semval", "idx", "clock", "nop", "seg", "cost", "lat", "aset")

    def __init__(self, eng, fn, reads, writes, is_dma):
        self.eng = eng
        self.fn = fn
        self.reads = reads
        self.writes = writes
        self.is_dma = is_dma
        self.deps = []
        self.signaled = False
        self.sem = None
        self.semval = None
        self.clock = None
        self.nop = False


class _FakeEng:
    def __init__(self, eng, is_dma):
        self.eng = eng
        self.is_dma = is_dma
        self.cost = 0.0
        self.lat = 0.0
        self.aset = None

    def __getattr__(self, name):
        def f(*a, **k):
            out = k.get("out", a[0] if a else None)
            try:
                n = float(out.free_size())
            except Exception:
                n = 64.0
            if name == "dma_start":
                nb_ = float(out.nbytes()) if hasattr(out, "nbytes") else 4096.0
                self.cost += 150.0 if self.eng == "sp" else 900.0
                self.lat = 2500.0 + nb_ * 128.0 / 150.0 / 128.0
                return self
            if name == "matmul":
                r = k.get("rhs")
                n = float(r.free_size())
                self.cost += max(n, 64.0) / 2.2 + 12.0
                self.lat = 250.0
                return self
            if name == "transpose":
                r = k.get("in_")
                self.cost += max(float(r.free_size()), 64.0) / 2.2 + 12.0
                self.lat = 250.0
                return self
            if name == "activation":
                fn_ = str(k.get("func"))
                for key_, set_ in (("Exp", "le"), ("Ln", "le"), ("Silu", "si"), ("Tanh", "si"), ("Sigmoid", "sg"), ("Sqrt", "sq")):
                    if fn_.endswith(key_):
                        self.aset = set_
                self.cost += (224.0 + n) / 1.2 + (90.0 if k.get("accum_out") is not None else 0.0)
                self.lat = 100.0
                return self
            fac = {"tensor_copy": 0.6, "tensor_scalar": 0.6, "memset": 0.5, "tensor_tensor": 1.0,
                   "scalar_tensor_tensor": 1.0, "tensor_tensor_scan": 2.0, "reciprocal": 3.8}.get(name, 1.0)
            c = (70.0 + n * fac) / 0.96
            if self.eng == "pool":
                c *= 1.7
            self.cost += c
            self.lat = 100.0
            return self
        return f


class Prog:
    def __init__(self, nc, n_dma_sems=10):
        self.nc = nc
        self.ops = []
        self.last_w = {}
        self.readers = {}
        self.n_dma_sems = n_dma_sems
        self.pending_dma = []
        self.nbar = 0
        self.bar_fns = {}
        self.seg = 0
        self.window = 900
        self.reorder = True

    def op(self, eng, fn, r=(), w=(), dma=False):
        o = _Op(eng, fn, tuple(r), tuple(w), dma)
        o.idx = len(self.ops)
        deps = set()
        for t in o.reads:
            x = self.last_w.get(t)
            if x is not None:
                deps.add(x)
        for t in o.writes:
            x = self.last_w.get(t)
            if x is not None:
                deps.add(x)
            for rr in self.readers.get(t, ()):
                deps.add(rr)
        for t in o.reads:
            self.readers.setdefault(t, []).append(o)
        for t in o.writes:
            self.last_w[t] = o
            self.readers[t] = []
        o.deps = sorted(deps, key=lambda d: d.idx)
        o.seg = self.seg
        self.ops.append(o)
        if dma:
            self.pending_dma.append(o)
        return o

    def barrier(self):
        n = self.nbar
        self.nbar += 1
        self.seg += 1
        toks = []
        for e in ("pe", "act", "dve", "pool"):
            fn, r_, w_ = self.bar_fns[e]
            self.op(e, fn, r_, list(w_) + [("bar", n, e)])
            toks.append(("bar", n, e))
        pend = list(self.pending_dma)
        for e in ENGS:
            o = self.op(e, lambda eng: None, toks, ())
            o.nop = True
            o.deps = o.deps + pend
        self.pending_dma = []
        self.last_w = {}
        self.readers = {}
        self.seg += 1

    def schedule(self):
        for o in self.ops:
            fk = _FakeEng(o.eng, o.is_dma)
            try:
                o.fn(fk)
            except Exception:
                fk.cost = 300.0
            o.cost = max(fk.cost, 20.0)
            o.lat = fk.lat
            o.aset = fk.aset
        segs = {}
        for o in self.ops:
            segs.setdefault(o.seg, []).append(o)
        order = []
        for sg in sorted(segs):
            ops = segs[sg]
            if len(ops) < 3 or any(o.nop for o in ops) or not self.reorder:
                order.extend(ops)
                continue
            inseg = set(id(o) for o in ops)
            users = {id(o): [] for o in ops}
            nrem = {}
            for o in ops:
                c = 0
                for d in o.deps:
                    if id(d) in inseg:
                        users[id(d)].append(o)
                        c += 1
                nrem[id(o)] = c
            t_free = {e: 0.0 for e in ENGS}
            cur_set = [None]
            fin = {}
            drt = {}
            ready = {e: [] for e in ENGS}
            for o in ops:
                if nrem[id(o)] == 0:
                    drt[id(o)] = 0.0
                    ready[o.eng].append(o)
            done = set()
            pos = 0
            nleft = len(ops)
            while nleft:
                while id(ops[pos]) in done:
                    pos += 1
                lim = ops[pos].idx + self.window
                best = None
                for e in ENGS:
                    tf = t_free[e]
                    for o in ready[e]:
                        if o.idx >= lim:
                            continue
                        est = drt[id(o)]
                        if est < tf:
                            est = tf
                        if o.aset is not None and o.aset != cur_set[0]:
                            est += 1300.0
                        key = (est, o.idx)
                        if best is None or key < best[0]:
                            best = (key, o)
                (est, _), o = best
                ready[o.eng].remove(o)
                if o.aset is not None:
                    cur_set[0] = o.aset
                if o.is_dma:
                    t_free[o.eng] = est + o.cost
                    fin[id(o)] = est + o.cost + o.lat
                else:
                    t_free[o.eng] = est + o.cost
                    fin[id(o)] = est + o.cost + o.lat
                done.add(id(o))
                order.append(o)
                nleft -= 1
                for u in users[id(o)]:
                    nrem[id(u)] -= 1
                    if nrem[id(u)] == 0:
                        t = 0.0
                        for d in u.deps:
                            if id(d) in inseg:
                                ft = fin[id(d)] + (120.0 if d.eng != u.eng else 0.0)
                                if d.eng == u.eng and not d.is_dma:
                                    ft = fin[id(d)] - d.lat
                                if ft > t:
                                    t = ft
                        drt[id(u)] = t
                        ready[u.eng].append(u)
        for i, o in enumerate(order):
            o.idx = i
        self.ops = order

    def pe(self, fn, r=(), w=()):
        return self.op("pe", fn, r, w)

    def act(self, fn, r=(), w=()):
        return self.op("act", fn, r, w)

    def dve(self, fn, r=(), w=()):
        return self.op("dve", fn, r, w)

    def pool(self, fn, r=(), w=()):
        return self.op("pool", fn, r, w)

    def dma(self, eng, fn, r=(), w=()):
        return self.op(eng, fn, r, w, dma=True)

    def emit(self):
        nc = self.nc
        self.schedule()
        ops = self.ops

        def need_wait(o, d):
            if d.nop:
                assert d.eng == o.eng
                return False
            if d.is_dma or d.eng != o.eng:
                return True
            if o.eng == "pe":
                return False
            return True

        for o in ops:
            o.deps = [d for d in o.deps if need_wait(o, d)]
            for d in o.deps:
                d.signaled = True
        per_eng = {e: [x for x in ops if x.eng == e] for e in ENGS}
        stack = contextlib.ExitStack()
        tick_sem = {}
        for e in ENGS:
            if any(x.signaled and not x.is_dma for x in per_eng[e]):
                tick_sem[e] = stack.enter_context(nc.semaphore("tk_" + e))
        for e in ENGS:
            dl = [x for x in per_eng[e] if x.is_dma]
            if not dl:
                continue
            n = min(self.n_dma_sems, len(dl))
            sems = [stack.enter_context(nc.semaphore("dq_%s_%d" % (e, i))) for i in range(n)]
            cnt = [0] * n
            prev = [None] * n
            for k, x in enumerate(dl):
                i = k % n
                cnt[i] += 1
                x.sem = sems[i]
                x.semval = 16 * cnt[i]
                x.signaled = True
                if prev[i] is not None:
                    x.deps = x.deps + [prev[i]]
                prev[i] = x
        for e in ENGS:
            t = 0
            for x in per_eng[e]:
                if x.is_dma or not x.signaled:
                    continue
                t += 1
                x.sem = tick_sem[e]
                x.semval = t
        known = {e: {} for e in ENGS}
        plan = {e: [] for e in ENGS}
        nw = 0
        for x in ops:
            kn = known[x.eng]
            need = {}
            for d in x.deps:
                key = id(d.sem)
                if kn.get(key, 0) >= d.semval:
                    continue
                need[key] = (d.sem, d.semval)
                kn[key] = d.semval
                if d.clock:
                    for k2, v2 in d.clock.items():
                        if kn.get(k2, 0) < v2:
                            kn[k2] = v2
            waits = [v for k_, v in need.items() if kn.get(k_, 0) <= v[1]]
            if x.signaled and not x.is_dma:
                c = dict(kn)
                c[id(x.sem)] = x.semval
                x.clock = c
            plan[x.eng].append((waits, x))
            nw += len(waits)
        self.stats = {e: len(per_eng[e]) for e in ENGS}
        self.stats["waits"] = nw

        def make(e):
            def body(eng):
                for waits, x in plan[e]:
                    for s_, v_ in waits:
                        eng.wait_ge(s_, v_)
                    ins = x.fn(eng)
                    if x.signaled:
                        ins.then_inc(x.sem, 16 if x.is_dma else 1)
            return body

        with stack:
            with nc.Block() as block:
                if plan["pe"]:
                    block.tensor(make("pe"))
                if plan["act"]:
                    block.scalar(make("act"))
                if plan["dve"]:
                    block.vector(make("dve"))
                if plan["pool"]:
                    block.gpsimd(make("pool"))
                if plan["sp"]:
                    block.sync(make("sp"))


def build_program(L=2, dbg=None, nchunks=NT):
    nc = bass.Bass("TRN2", target_bir_lowering=False)
    x_d = nc.dram_tensor("x", [S, D], F32, kind="ExternalInput").ap()
    win_d = nc.dram_tensor("w_in", [L, D, DIN], F32, kind="ExternalInput").ap()
    wout_d = nc.dram_tensor("w_out", [L, MIX, D], F32, kind="ExternalInput").ap()
    pp_d = nc.dram_tensor("pp", [L, 128, NPP], F32, kind="ExternalInput").ap()
    lw_d = nc.dram_tensor("lw", [L, 128, 8, 128], F32, kind="ExternalInput").ap()
    cc_d = nc.dram_tensor("cc", [128, NCC], F32, kind="ExternalInput").ap()
    out_d = nc.dram_tensor("out", [S, D], F32, kind="ExternalOutput").ap()
    hmid_d = nc.dram_tensor("hmid", [S, D], F32, kind="Internal").ap()
    utd_d = nc.dram_tensor("utd", [128, 8, S], BF16, kind="Internal").ap()
    yatd_d = nc.dram_tensor("yatd", [128, 4, S], BF16, kind="Internal").ap()
    ymd_d = nc.dram_tensor("ymd", [128, 12, S], BF16, kind="Internal").ap()
    wbf_d = nc.dram_tensor("wbf", [L, 128, 8, DIN], BF16, kind="Internal").ap()
    wobf_d = nc.dram_tensor("wobf", [L, 128, 16, D], BF16, kind="Internal").ap()
    dbg_out = {}
    st = contextlib.ExitStack()

    def mk(stk, sfx=""):
        def sb(name, shape, dt=F32):
            return stk.enter_context(nc.sbuf_tensor(name + sfx, shape, dt))
        return sb

    with st:
        P = Prog(nc)
        sb = mk(st)
        ccb = sb("ccb", [128, NCC], BF16)
        ccf = sb("ccf", [128, 256], F32)
        ppt = sb("ppt", [128, NPP], F32)
        lwb = sb("lwb", [128, 8, 128], BF16)
        small = sb("small", [128, 64], F32)
        onesrow = sb("onesrow", [16, 512], F32)
        scr = sb("scr", [128, 8], F32)
        ptr = st.enter_context(nc.psum_tensor("ptr", [128, 8, 128], BF16))
        pj = [st.enter_context(nc.psum_tensor("pj%d" % i, [128, 512], F32)) for i in range(2)]
        pm = st.enter_context(nc.psum_tensor("pm", [128, 512], F32))
        pa = [st.enter_context(nc.psum_tensor("pa%d" % i, [128, 512], F32)) for i in range(4)]

        ident = ccf[:, 0:128]
        identb = ccb[:, CC_ID:CC_ID + 128]
        trib = ccb[:, CC_TRI:CC_TRI + 128]
        mneg4 = ccb[:, CC_MNEG:CC_MNEG + 512]
        BDb = ccb[:, CC_BD:CC_BD + 128]
        zero128 = ccb[:, CC_ZERO:CC_ZERO + 128]
        selH = ccb[:, CC_SELH:CC_SELH + 1024].rearrange("p (h c) -> p h c", h=8)
        selR = ccb[:, CC_SELR:CC_SELR + 2048].rearrange("p (h c) -> p h c", h=16)
        epsc = small[:, 40:41]
        ptrfA = ptr[:, :, :].rearrange("p a b -> p (a b)").bitcast(F32)

        P.bar_fns = {
            "pe": (lambda e: e.matmul(pm[0:8, 0:8], lhsT=identb[0:8, 0:8], rhs=identb[0:8, 0:8], start=True, stop=True),
                   ["ccb"], ["pm"]),
            "act": (lambda e: e.activation(out=scr[:, 0:1], in_=scr[:, 1:2], func=AF.Copy), [], ["scr_a"]),
            "dve": (lambda e: e.memset(scr[:, 2:3], 0.0), [], ["scr_d"]),
            "pool": (lambda e: e.memset(scr[:, 4:5], 0.0), [], ["scr_p"]),
        }

        def dump(name, ap, shape, dt, rtoks):
            if dbg is None or name not in dbg:
                return
            t = nc.dram_tensor("dbg_" + name, list(shape), dt, kind="ExternalOutput").ap()
            dbg_out[name] = t
            P.dma("sp", lambda e: e.dma_start(out=t, in_=ap), r=rtoks, w=["dbg_" + name])

        P.dma("pool", lambda e: e.dma_start(out=ccb[:, :], in_=cc_d[:, :]), w=["ccb"])
        P.dma("sp", lambda e: e.dma_start(out=ccf[:, :], in_=cc_d[:, 0:256]), w=["ccf"])
        P.dve(lambda e: e.memset(onesrow[:, :], 1.0), w=["ones"])
        P.dve(lambda e: e.memset(scr[:, :], 0.0), w=["scr_a", "scr_d", "scr_p"])
        P.dve(lambda e: e.memset(small[:, :], 0.0), w=["small"])
        P.dve(lambda e: e.memset(epsc, EPS), r=["small"], w=["eps"])
        P.barrier()

        precast = []
        for l_ in range(L):
            wv_ = win_d[l_].rearrange("(k p) c -> p k c", p=128)
            ov_ = wout_d[l_].rearrange("(k p) c -> p k c", p=128)
            c_lo = 2056 if l_ == 0 else 0
            for k in range(8):
                precast.append((wbf_d[l_][:, k, c_lo:DIN], wv_[:, k, c_lo:DIN]))
            for k in range(0, 16, 4):
                precast.append((wobf_d[l_][:, k:k + 4, :], ov_[:, k:k + 4, :]))

        def do_layer(l):
            src_d = x_d if l == 0 else hmid_d
            dst_d = out_d if l == L - 1 else hmid_d
            winv = win_d[l].rearrange("(k p) c -> p k c", p=128)
            woutv = wout_d[l].rearrange("(k p) c -> p k c", p=128)
            P.dma("sp", lambda e, l=l: e.dma_start(out=ppt[:, :], in_=pp_d[l]), w=["ppt"])
            P.dma("pool", lambda e, l=l: e.dma_start(out=lwb[:, :, :], in_=lw_d[l]), w=["lwb"])
            P.dve(lambda e: e.tensor_scalar(out=small[:, 0:1], in0=ppt[:, PP_GQ:PP_GQ + 1], scalar1=0.125,
                                            scalar2=None, op0=ALU.mult), r=["ppt"], w=["small"])
            P.dve(lambda e: e.tensor_copy(out=small[:, 1:2], in_=ppt[:, PP_GK:PP_GK + 1]), r=["ppt"], w=["small"])
            P.dve(lambda e: e.tensor_scalar(out=small[0:8, 2:3], in0=ppt[0:8, PP_FB:PP_FB + 1], scalar1=-1.0,
                                            scalar2=None, op0=ALU.mult), r=["ppt"], w=["small"])
            P.act(lambda e: e.activation(out=small[0:16, 3:4], in_=ppt[0:16, PP_ALOG:PP_ALOG + 1], func=AF.Exp),
                  r=["ppt"], w=["small3"])
            P.dve(lambda e: e.tensor_scalar(out=small[0:16, 3:4], in0=small[0:16, 3:4], scalar1=-1.0,
                                            scalar2=None, op0=ALU.mult), r=["small3"], w=["small3"])
            P.act(lambda e: e.activation(out=small[:, 4:8], in_=ppt[:, PP_LAM:PP_LAM + 4], func=AF.Exp, scale=-1.0),
                  r=["ppt"], w=["small4"])
            P.act(lambda e: e.activation(out=small[:, 4:8], in_=small[:, 4:8], func=AF.Ln, bias=1.0),
                  r=["small4"], w=["small4"])
            P.dve(lambda e: e.tensor_scalar(out=small[:, 8:12], in0=small[:, 4:8], scalar1=-16.0,
                                            scalar2=None, op0=ALU.mult), r=["small4"], w=["small8"])
            P.dve(lambda e: e.tensor_scalar(out=small[:, 4:8], in0=small[:, 4:8], scalar1=-8.0,
                                            scalar2=None, op0=ALU.mult), r=["small4", "small8"], w=["small4"])
            P.dve(lambda e: e.tensor_scalar(out=small[:, 12:16], in0=small[:, 4:8], scalar1=0.5,
                                            scalar2=None, op0=ALU.mult), r=["small4"], w=["small12"])
            P.dve(lambda e: e.tensor_scalar(out=small[:, 44:48], in0=ppt[:, PP_BA:PP_BA + 4], scalar1=0.5,
                                            scalar2=None, op0=ALU.mult), r=["ppt"], w=["small44"])
            P.dve(lambda e: e.tensor_scalar(out=small[:, 48:52], in0=ppt[:, PP_BX:PP_BX + 4], scalar1=0.5,
                                            scalar2=None, op0=ALU.mult), r=["ppt"], w=["small48"])
            gqs = small[:, 0:1]
            gks = small[:, 1:2]
            nfb = small[0:8, 2:3]
            Acol = small[0:16, 3:4]
            P.barrier()

            with contextlib.ExitStack() as sa:
                sb = mk(sa, "_l%d" % l)
                arena = sb("arenaA", [128, 49152], BF16)
                Kc = arena[:, 0:16384].rearrange("p (a t) -> p a t", a=4)
                Vc = arena[:, 16384:49152].rearrange("p (j h c) -> p j h c", j=32, h=8)
                WA = sb("WA", [128, 8, 2056], BF16)
                hxa = sb("hxa", [128, 2, 1024], F32)
                junk = sb("junkA", [128, 1024], BF16)
                un = [sb("unA%d" % i, [128, 1024], BF16) for i in range(2)]
                uT = sb("uTA", [128, 8, 512], BF16)
                qz = sb("qz", [128, 8, 512], BF16)
                zs = sb("zs", [128, 4, 512], BF16)
                sq = [sb("sq%d" % i, [128, 512], BF16) for i in range(2)]
                sd = [sb("sd%d" % i, [128, 512], F32) for i in range(2)]
                qraw = [sb("qraw%d" % i, [128, 512], BF16) for i in range(2)]
                lfn = sb("lfn", [8, 512], F32)
                cn = [sb("cn%d" % i, [8, 512], F32) for i in range(2)]
                chat = sb("chat", [128, 512], BF16)
                cbc = sb("cbc", [8, 128], F32)
                cn_tok = sb("cn_tok", [128, 32, 8], F32)
                cref = sb("cref", [128, 8], F32)
                biasT = sb("biasT", [128, 32, 8], F32)
                pT = [sb("pT%d" % i, [128, 512], BF16) for i in range(4)]
                rl = sb("rl", [128, 512], F32)
                yat = sb("yat", [128, 4, 512], BF16)

                for k in range(8):
                    if l == 0:
                        P.dma("pool", lambda e, k=k: e.dma_start(out=WA[:, k, :], in_=winv[:, k, 0:2056]), w=["WA%d" % k])
                    else:
                        P.dma("sp", lambda e, k=k: e.dma_start(out=WA[:, k, :], in_=wbf_d[l][:, k, 0:2056]), w=["WA%d" % k])
                WAt = ["WA%d" % k for k in range(8)]
                gbc = ppt[:, PP_G:PP_G + 8].unsqueeze(2).to_broadcast([128, 8, 128])
                P.pool(lambda e: e.memset(qz[:, :, :], 0.0), w=["qn"])
                P.pool(lambda e: e.memset(chat[:, :], 0.0), w=["chat"])
                for h in range(8):
                    lo1 = 64 if h % 2 == 0 else 0
                    P.pool(lambda e, h=h, lo1=lo1: e.memset(Vc[:, :, h, lo1:lo1 + 64], 1.0), w=["Vc"])
                it = 0
                for T in range(nchunks):
                    t0 = 512 * T
                    nb = 4 * T + 4
                    for i in range(4):
                        r0 = t0 + 128 * i
                        hb = i % 2
                        P.dma("sp", lambda e, r0=r0, hb=hb: e.dma_start(out=hxa[:, hb, :], in_=src_d[r0:r0 + 128, :]),
                              w=["hxa%d" % hb])
                        P.act(lambda e, i=i, hb=hb: e.activation(out=junk[:, :], in_=hxa[:, hb, :], func=AF.Square,
                                                                 accum_out=small[:, 16 + i:17 + i]),
                              r=["hxa%d" % hb], w=["junk", "ss%d" % i])
                        P.act(lambda e, i=i: e.activation(out=small[:, 20 + i:21 + i], in_=small[:, 16 + i:17 + i], func=AF.Ln,
                                                          scale=1.0 / D, bias=epsc), r=["ss%d" % i], w=["rs0%d" % i])
                        P.act(lambda e, i=i: e.activation(out=small[:, 24 + i:25 + i], in_=small[:, 20 + i:21 + i], func=AF.Exp,
                                                          scale=-0.5), r=["rs0%d" % i], w=["rs%d" % i])
                        u = un[i % 2]
                        P.dve(lambda e, i=i, u=u, hb=hb: e.tensor_scalar(out=u[:, :], in0=hxa[:, hb, :],
                                                                         scalar1=small[:, 24 + i:25 + i], scalar2=None,
                                                                         op0=ALU.mult), r=["hxa%d" % hb, "rs%d" % i], w=[u.name])

                        def tr(e, u=u):
                            ins = None
                            for k in range(8):
                                ins = e.transpose(out=ptr[:, k, :], in_=u[:, 128 * k:128 * k + 128], identity=identb)
                            return ins
                        P.pe(tr, r=[u.name], w=["ptr"])
                        P.dve(lambda e, i=i: e.tensor_tensor(out=uT[:, :, 128 * i:128 * i + 128], in0=ptr[:, :, :], in1=gbc, op=ALU.mult),
                              r=["ptr"], w=["uT"])
                    P.dma("sp", lambda e, t0=t0: e.dma_start(out=utd_d[:, :, t0:t0 + 512], in_=uT[:, :, :]),
                          r=["uT"], w=["utd"])
                    if l == 0:
                        per = (len(precast) + nchunks - 1) // nchunks
                        for (o_, i_) in precast[per * T:per * T + per]:
                            P.dma("pool", lambda e, o_=o_, i_=i_: e.dma_start(out=o_, in_=i_), w=["wbf%d" % id(o_)])

                    def proj_fm(pt, c0, m):
                        def f(e):
                            ins = None
                            for k in range(8):
                                ins = e.matmul(pt[0:m, :], lhsT=WA[:, k, c0:c0 + m], rhs=uT[:, k, :],
                                               start=(k == 0), stop=(k == 7))
                            return ins
                        return f

                    P.pe(proj_fm(pm, 2048, 8), r=WAt + ["uT"], w=["pm"])
                    P.act(lambda e: e.activation(out=lfn[:, :], in_=pm[0:8, :], func=AF.Exp, scale=-1.0, bias=nfb),
                          r=["pm"], w=["lfn"])
                    P.act(lambda e: e.activation(out=lfn[:, :], in_=lfn[:, :], func=AF.Ln, bias=1.0),
                          r=["lfn"], w=["lfn"])
                    cnc = cn[T % 2]
                    cnp = cn[(T + 1) % 2]
                    P.dve(lambda e, T=T, cnc=cnc, cnp=cnp: e.tensor_tensor_scan(
                        out=cnc[:, :], data0=onesrow[0:8, :], data1=lfn[:, :],
                        initial=(0.0 if T == 0 else cnp[:, 511:512]), op0=ALU.mult, op1=ALU.add),
                        r=["lfn", cnp.name], w=[cnc.name])
                    P.dve(lambda e, cnc=cnc: e.tensor_scalar(out=chat[0:8, :], in0=cnc[:, :], scalar1=-1.0,
                                                             scalar2=cnc[:, 0:1], op0=ALU.mult, op1=ALU.add),
                          r=[cnc.name], w=["chat"])
                    P.dve(lambda e, cnc=cnc: e.tensor_copy(out=cbc[:, :], in_=cnc[:, 0:1].to_broadcast([8, 128])),
                          r=[cnc.name], w=["cbc"])

                    def trc(e, cnc=cnc):
                        for i in range(4):
                            e.transpose(out=pm[:, 32 + 8 * i:40 + 8 * i], in_=cnc[:, 128 * i:128 * i + 128],
                                        identity=ident[0:8, 0:8])
                        return e.transpose(out=pm[:, 64:72], in_=cbc[:, :], identity=ident[0:8, 0:8])
                    P.pe(trc, r=[cnc.name, "cbc"], w=["pm"])
                    P.dve(lambda e, T=T: e.tensor_copy(out=cn_tok[:, 4 * T:4 * T + 4, :],
                                                       in_=pm[:, 32:64].rearrange("p (a b) -> p a b", b=8)),
                          r=["pm"], w=["cn_tok"])
                    P.dve(lambda e: e.tensor_copy(out=cref[:, :], in_=pm[:, 64:72]), r=["pm"], w=["cref"])
                    P.dve(lambda e, nb=nb: e.tensor_tensor(out=biasT[:, 0:nb, :], in0=cn_tok[:, 0:nb, :],
                                                           in1=cref[:, :].unsqueeze(1).to_broadcast([128, nb, 8]),
                                                           op=ALU.subtract), r=["cn_tok", "cref"], w=["biasT"])

                    def qk_group(c0, gcol, dst, dtoks, idx, split=None):
                        pt = pj[idx % 2]
                        s_ = sq[idx % 2]
                        d_ = sd[idx % 2]
                        qr_ = qraw[idx % 2]
                        P.pe(proj_fm(pt, c0, 128), r=WAt + ["uT"], w=[pt.name])
                        P.act(lambda e: e.activation(out=s_[:, :], in_=pt[:, :], func=AF.Square), r=[pt.name], w=[s_.name])
                        P.dve(lambda e: e.tensor_copy(out=qr_[:, :], in_=pt[:, :]), r=[pt.name, s_.name], w=[qr_.name])
                        P.pe(lambda e: e.matmul(ptrfA[:, :], lhsT=BDb, rhs=s_[:, :], start=True, stop=True),
                             r=[s_.name], w=["ptr"])
                        P.act(lambda e: e.activation(out=d_[:, :], in_=ptrfA[:, :], func=AF.Ln, bias=epsc), r=["ptr"], w=[d_.name])
                        P.act(lambda e: e.activation(out=d_[:, :], in_=d_[:, :], func=AF.Exp, scale=-0.5), r=[d_.name], w=[d_.name])
                        if split is None:
                            P.dve(lambda e: e.scalar_tensor_tensor(out=dst, in0=qr_[:, :], scalar=gcol, in1=d_[:, :],
                                                                   op0=ALU.mult, op1=ALU.mult),
                                  r=[qr_.name, d_.name], w=dtoks)
                        else:
                            for hh in range(2):
                                rs_ = slice(64 * hh, 64 * hh + 64)
                                P.dve(lambda e, rs_=rs_, hh=hh: e.scalar_tensor_tensor(
                                    out=qz[rs_, 2 * split + hh, :], in0=qr_[rs_, :], scalar=gcol[rs_, :], in1=d_[rs_, :],
                                    op0=ALU.mult, op1=ALU.mult), r=[qr_.name, d_.name], w=dtoks)
                    gi = 0
                    for p_ in range(4):
                        qk_group(512 + 128 * p_, gks, Kc[:, p_, t0:t0 + 512], ["KV"], gi)
                        gi += 1
                    for p_ in range(4):
                        qk_group(128 * p_, gqs, None, ["qn"], gi, split=p_)
                        gi += 1
                    for p_ in range(4):
                        pt = pj[gi % 2]
                        gi += 1
                        P.pe(proj_fm(pt, 1536 + 128 * p_, 128), r=WAt + ["uT"], w=[pt.name])
                        P.act(lambda e, pt=pt, p_=p_: e.activation(out=zs[:, p_, :], in_=pt[:, :], func=AF.Silu),
                              r=[pt.name], w=["zs"])
                    for i in range(4):
                        pt = pj[gi % 2]
                        gi += 1

                        def vproj(e, pt=pt, i=i):
                            ins = None
                            for k in range(8):
                                ins = e.matmul(pt[:, :], lhsT=uT[:, k, 128 * i:128 * i + 128], rhs=WA[:, k, 1024:1536],
                                               start=(k == 0), stop=(k == 7))
                            return ins
                        P.pe(vproj, r=WAt + ["uT"], w=[pt.name])
                        j = 4 * T + i
                        ptv = pt[:, :].rearrange("p (a two c) -> p a two c", two=2, c=64)
                        vcv = Vc[:, j, :, :].rearrange("p (a two) c -> p a two c", two=2)
                        P.dve(lambda e, ptv=ptv, vcv=vcv: e.tensor_copy(out=vcv[:, :, 0, 0:64], in_=ptv[:, :, 0, :]),
                              r=[pt.name], w=["KV"])
                        P.dve(lambda e, ptv=ptv, vcv=vcv: e.tensor_copy(out=vcv[:, :, 1, 64:128], in_=ptv[:, :, 1, :]),
                              r=[pt.name], w=["KV"])

                    if T == 0:
                        jorder = [0, 1, 2, 3]
                    else:
                        jorder = [4 * T, 4 * T + 1, 4 * T + 2, 4 * T + 3] + list(range(4 * T))
                    for h in range(8):
                        p_ = h // 2
                        lo = 64 * (h % 2)
                        po = (pa[3], pm)[h % 2]
                        for n_, j in enumerate(jorder):
                            c0 = 0 if j < 4 * T else 128 * (j - 4 * T)
                            pst_ = pa[it % 3]
                            pt_ = pT[it % 4]
                            it += 1

                            def f(e, j=j, c0=c0, pst_=pst_, h=h, p_=p_):
                                e.matmul(pst_[:, c0:512], lhsT=Kc[:, p_, 128 * j:128 * j + 128],
                                         rhs=qz[:, h, c0:512], start=True, stop=False)
                                return e.matmul(pst_[:, c0:512], lhsT=selH[:, h, :], rhs=chat[:, c0:512],
                                                start=False, stop=True)
                            P.pe(f, r=["KV", "Vc", "qn", "chat"], w=[pst_.name])
                            P.act(lambda e, j=j, c0=c0, pst_=pst_, pt_=pt_, h=h: e.activation(
                                out=pt_[:, c0:512], in_=pst_[:, c0:512], func=AF.Exp, bias=biasT[:, j, h:h + 1]),
                                r=[pst_.name, "biasT"], w=[pt_.name])
                            if j >= 4 * T:
                                P.pool(lambda e, c0=c0, pt_=pt_: e.tensor_tensor(out=pt_[:, c0:c0 + 128], in0=pt_[:, c0:c0 + 128],
                                                                                 in1=trib, op=ALU.mult),
                                       r=[pt_.name], w=[pt_.name])
                            first = (n_ == 0)
                            last = (n_ == len(jorder) - 1) and T > 0
                            P.pe(lambda e, j=j, c0=c0, pt_=pt_, h=h, po=po, first=first, last=last: e.matmul(
                                po[:, c0:512], lhsT=Vc[:, j, h, :], rhs=pt_[:, c0:512], start=first, stop=last),
                                r=["KV", "Vc", pt_.name], w=[po.name])
                        if T == 0:
                            P.pe(lambda e, h=h, po=po: e.matmul(po[:, :], lhsT=zero128, rhs=qz[:, h, :], start=False, stop=True),
                                 r=["qn"], w=[po.name])
                        ol = po[lo:lo + 64, :]
                        ll = po[64 - lo:128 - lo, :]
                        P.dve(lambda e, ll=ll, lo=lo: e.reciprocal(out=rl[lo:lo + 64, :], in_=ll), r=[po.name], w=["rl"])
                        P.dve(lambda e, lo=lo, p_=p_: e.tensor_tensor(out=rl[lo:lo + 64, :], in0=rl[lo:lo + 64, :],
                                                                     in1=zs[lo:lo + 64, p_, :], op=ALU.mult),
                              r=["rl", "zs"], w=["rl"])
                        P.dve(lambda e, ol=ol, lo=lo, p_=p_: e.tensor_tensor(out=yat[lo:lo + 64, p_, :], in0=ol,
                                                                            in1=rl[lo:lo + 64, :], op=ALU.mult),
                              r=[po.name, "rl"], w=["yat"])
                    P.dma("sp", lambda e, t0=t0: e.dma_start(out=yatd_d[:, :, t0:t0 + 512], in_=yat[:, :, :]),
                          r=["yat"], w=["yatd"])
                    if l == 0 and T == 0:
                        dump("uT", uT[:, :, :], [128, 8, 512], BF16, ["uT"])
                        dump("qz", qz[:, :, :], [128, 8, 512], BF16, ["qn"])
                        dump("cn", cn[0][:, :], [8, 512], F32, [cn[0].name])
                        dump("yat", yat[:, :, :], [128, 4, 512], BF16, ["yat"])
                    if l == 0 and T == 1:
                        dump("yat1", yat[:, :, :], [128, 4, 512], BF16, ["yat"])
                P.barrier()

            with contextlib.ExitStack() as sB:
                sb = mk(sB, "_l%d" % l)
                WB = sb("WB", [128, 8, 3600], BF16)
                uTb = sb("uTB", [128, 8, 512], BF16)
                junkb = sb("junkB", [128, 512], BF16)
                rawb = [sb("rawb%d" % i, [128, 515], BF16) for i in range(3)]
                hist = sb("hist", [128, 16, 3], BF16)
                dg = sb("dg", [128, 64, 128], BF16)
                xc = sb("xc", [128, 12, 512], BF16)
                xcl = sb("xcl", [128, 512], F32)
                xclb = sb("xclb", [128, 512], BF16)
                zl = sb("zl", [128, 4, 512], BF16)
                zsd = sb("zsd", [128, 4, 1024], BF16)
                dtt = sb("dtt", [16, 512], F32)
                acs = sb("acs", [16, 512], F32)
                lastb = sb("lastb", [16, 512], F32)
                r1 = sb("r1", [16, 512], F32)
                hib = sb("hib", [16, 512], BF16)
                D2 = sb("D2", [128, 512], BF16)
                nD2 = sb("nD2", [128, 512], BF16)
                tok3 = sb("tok3", [128, 12, 16], F32)
                ea_tok = sb("ea_tok", [128, 4, 16], F32)
                eal_tok = sb("eal_tok", [128, 4, 16], F32)
                wst = sb("wst", [128, 4, 16], F32)
                xs_tok = sb("xs_tok", [128, 1024], BF16)
                b_tok = sb("b_tok", [128, 2, 128], BF16)
                cb = sb("cb", [128, 2, 128], BF16)
                Lt = [sb("Lt%d" % i, [128, 512], BF16) for i in range(2)]
                Mt2 = [sb("Mt%d" % i, [128, 16, 128], BF16) for i in range(2)]
                xdt2 = [sb("xdt%d" % i, [128, 1024], BF16) for i in range(2)]
                xdd2 = [sb("xdd%d" % i, [128, 1024], BF16) for i in range(2)]
                xDs2 = [sb("xDs%d" % i, [128, 1024], BF16) for i in range(2)]
                prev = sb("prev", [128, 2, 512], F32)
                prevb = sb("prevb", [128, 2, 512], BF16)
                ysum2 = [sb("ysum%d" % i, [128, 512], F32) for i in range(2)]
                gated2 = [sb("gated%d" % i, [128, 512], F32) for i in range(2)]
                ysn2 = [sb("ysn%d" % i, [128, 512], BF16) for i in range(2)]
                mixT = sb("mixT", [128, 12, 512], BF16)
                lr = sb("lr", [128, 512], F32)
                li = sb("li", [128, 512], F32)
                la = sb("la", [128, 512], F32)
                lm = sb("lm", [128, 512], F32)
                hsb = [sb("hsb%d" % i, [128, 512], F32) for i in range(2)]
                hcar = sb("hcar", [128, 4], F32)

                for k in range(8):
                    P.dma("sp", lambda e, k=k: e.dma_start(out=WB[:, k, :], in_=wbf_d[l][:, k, 2056:DIN]), w=["WB%d" % k])
                WBt = ["WB%d" % k for k in range(8)]
                P.dve(lambda e: e.memset(hist[:, :, :], 0.0), w=["hist%d" % g for g in range(16)])
                P.dve(lambda e: e.memset(prev[:, :, :], 0.0), w=["prev0", "prev1"])
                P.dve(lambda e: e.memset(prevb[:, :, :], 0.0), w=["prevb0", "prevb1"])
                P.dve(lambda e: e.memset(D2[:, :], 0.0), w=["D2"])
                P.dve(lambda e: e.memset(nD2[:, :], 0.0), w=["nD2"])
                P.dve(lambda e: e.memset(hcar[:, :], 0.0), w=["hcar"])
                scw = ppt[:, PP_SCW:PP_SCW + 48].rearrange("p (g k) -> p g k", k=4)
                lcw = ppt[:, PP_LCW:PP_LCW + 16].rearrange("p (g k) -> p g k", k=4)
                for g in range(16):
                    wv_ = scw[:, g, :] if g < 12 else lcw[:, g - 12, :]
                    for k in range(4):
                        eng_ = P.pool if (4 * g + k) % 2 == 0 else P.dve
                        eng_(lambda e, g=g, k=k, wv_=wv_: e.tensor_scalar(out=dg[:, 4 * g + k, :], in0=identb, scalar1=wv_[:, k:k + 1],
                                                                         scalar2=None, op0=ALU.mult), w=["dg%d" % (4 * g + k)])
                Dbc = ppt[:, PP_DBC:PP_DBC + 16]
                gnb = ppt[:, PP_GN:PP_GN + 1024]
                pz, pyd, pyo, pst = pa
                pring = [pj[0], pj[1], pa[1], pa[2]]
                pmb = pm[:, 0:256].bitcast(BF16).rearrange("p (q c) -> p q c", q=4)
                ptrf = ptr[:, :, :].rearrange("p a b -> p (a b)").bitcast(F32)
                hcount = 0
                for T in range(nchunks):
                    t0 = 512 * T
                    P.dma("sp", lambda e, t0=t0: e.dma_start(out=uTb[:, :, :], in_=utd_d[:, :, t0:t0 + 512]), w=["uT"])

                    def proj_b(pt, c0, m):
                        def f(e):
                            ins = None
                            for k in range(8):
                                ins = e.matmul(pt[0:m, :], lhsT=WB[:, k, c0:c0 + m], rhs=uTb[:, k, :],
                                               start=(k == 0), stop=(k == 7))
                            return ins
                        return f
                    gi = 0
                    P.pe(proj_b(pm, 2560, 16), r=WBt + ["uT"], w=["pm"])
                    P.act(lambda e: e.activation(out=dtt[:, :], in_=pm[0:16, :], func=AF.Exp,
                                                 bias=ppt[0:16, PP_DTB:PP_DTB + 1]), r=["pm"], w=["dtt"])
                    P.act(lambda e: e.activation(out=dtt[:, :], in_=dtt[:, :], func=AF.Ln, bias=1.0), r=["dtt"], w=["dtt"])
                    P.dve(lambda e: e.tensor_scalar(out=r1[:, :], in0=dtt[:, :], scalar1=Acol, scalar2=None, op0=ALU.mult),
                          r=["dtt"], w=["r1"])
                    for c in range(4):
                        P.dve(lambda e, c=c: e.tensor_tensor_scan(
                            out=acs[:, 128 * c:128 * c + 128], data0=onesrow[0:16, 0:128], data1=r1[:, 128 * c:128 * c + 128],
                            initial=0.0, op0=ALU.mult, op1=ALU.add), r=["r1"], w=["acs"])
                    P.dve(lambda e: e.tensor_copy(
                        out=lastb[:, :].rearrange("p (c l) -> p c l", c=4),
                        in_=acs[:, :].rearrange("p (c l) -> p c l", c=4)[:, :, 127:128].to_broadcast([16, 4, 128])),
                        r=["acs"], w=["lastb"])

                    def trd(e):
                        ins = None
                        for c in range(4):
                            e.transpose(out=pm[:, 16 * c:16 * c + 16], in_=dtt[:, 128 * c:128 * c + 128], identity=ident[0:16, 0:16])
                            e.transpose(out=pm[:, 64 + 16 * c:80 + 16 * c], in_=acs[:, 128 * c:128 * c + 128], identity=ident[0:16, 0:16])
                            ins = e.transpose(out=pm[:, 128 + 16 * c:144 + 16 * c], in_=lastb[:, 128 * c:128 * c + 128],
                                              identity=ident[0:16, 0:16])
                        return ins
                    P.pe(trd, r=["dtt", "acs", "lastb"], w=["pm"])
                    P.dve(lambda e: e.tensor_copy(out=tok3[:, :, :], in_=pm[:, 0:192].rearrange("p (a b) -> p a b", b=16)),
                          r=["pm"], w=["tok3"])
                    P.act(lambda e: e.activation(out=ea_tok[:, :, :], in_=tok3[:, 4:8, :], func=AF.Exp), r=["tok3"], w=["ea_tok"])
                    P.act(lambda e: e.activation(out=eal_tok[:, :, :], in_=tok3[:, 8:12, :], func=AF.Exp), r=["tok3"], w=["eal_tok"])
                    P.dve(lambda e: e.tensor_tensor(out=wst[:, :, :], in0=tok3[:, 8:12, :], in1=tok3[:, 4:8, :], op=ALU.subtract),
                          r=["tok3"], w=["wst"])
                    P.act(lambda e: e.activation(out=wst[:, :, :], in_=wst[:, :, :], func=AF.Exp), r=["wst"], w=["wst"])
                    P.dve(lambda e: e.tensor_tensor(out=wst[:, :, :], in0=wst[:, :, :], in1=tok3[:, 0:4, :], op=ALU.mult),
                          r=["wst", "tok3"], w=["wst"])
                    P.dve(lambda e: e.tensor_copy(out=hib[:, :], in_=acs[:, :]), r=["acs"], w=["hib"])
                    P.dve(lambda e: e.tensor_tensor(out=r1[:, :], in0=acs[:, :], in1=hib[:, :], op=ALU.subtract),
                          r=["acs", "hib"], w=["r1"])
                    P.dve(lambda e: e.tensor_copy(out=D2[0:16, :], in_=hib[:, :]), r=["hib"], w=["D2"])
                    P.dve(lambda e: e.tensor_copy(out=D2[32:48, :], in_=r1[:, :]), r=["r1"], w=["D2"])
                    P.dve(lambda e: e.tensor_scalar(out=nD2[0:48, :], in0=D2[0:48, :], scalar1=-1.0, scalar2=None, op0=ALU.mult),
                          r=["D2"], w=["nD2"])
                    for m in range(4):
                        pt = pring[gi % 4]
                        gi += 1
                        P.pe(proj_b(pt, 3088 + 128 * m, 128), r=WBt + ["uT"], w=[pt.name])
                        P.act(lambda e, pt=pt, m=m: e.activation(out=zl[:, m, :], in_=pt[:, :], func=AF.Silu),
                              r=[pt.name], w=["zl"])
                    for g in range(16):
                        pt = pring[gi % 4]
                        gi += 1
                        c0 = 1024 + 128 * g if g < 12 else 2576 + 128 * (g - 12)
                        rb = rawb[g % 3]
                        P.pe(proj_b(pt, c0, 128), r=WBt + ["uT"], w=[pt.name])
                        P.dve(lambda e, g=g, rb=rb: e.tensor_copy(out=rb[:, 0:3], in_=hist[:, g, :]), r=["hist%d" % g], w=[rb.name])
                        P.act(lambda e, pt=pt, rb=rb: e.activation(out=rb[:, 3:515], in_=pt[:, :], func=AF.Copy),
                              r=[pt.name], w=[rb.name])
                        P.dve(lambda e, g=g, rb=rb: e.tensor_copy(out=hist[:, g, :], in_=rb[:, 512:515]), r=[rb.name], w=["hist%d" % g])
                        pc = pring[gi % 4]
                        gi += 1

                        def convmm(e, g=g, rb=rb, pc=pc):
                            ins = None
                            for k in range(4):
                                ins = e.matmul(pc[:, :], lhsT=dg[:, 4 * g + k, :], rhs=rb[:, k:k + 512], start=(k == 0), stop=(k == 3))
                            return ins
                        P.pe(convmm, r=[rb.name] + ["dg%d" % (4 * g + k) for k in range(4)], w=[pc.name])
                        if g < 12:
                            P.act(lambda e, g=g, pc=pc: e.activation(out=xc[:, g, :], in_=pc[:, :], func=AF.Silu,
                                                                     bias=ppt[:, PP_SCB + g:PP_SCB + g + 1]), r=[pc.name], w=["xc"])
                            continue
                        P.act(lambda e, g=g, pc=pc: e.activation(out=xcl[:, :], in_=pc[:, :], func=AF.Identity,
                                                                 bias=ppt[:, PP_LCB + g - 12:PP_LCB + g - 11]), r=[pc.name], w=["xcl"])
                        m = g - 12
                        P.act(lambda e: e.activation(out=xclb[:, :], in_=xcl[:, :], func=AF.Copy), r=["xcl"], w=["xclb"])
                        pga = pring[gi % 4]
                        pgx = pring[(gi + 1) % 4]
                        gi += 2
                        P.pe(lambda e, m=m, pga=pga: e.matmul(pga[:, :], lhsT=lwb[:, m, :], rhs=xclb[:, :], start=True, stop=True),
                             r=["xclb"], w=[pga.name])
                        P.pe(lambda e, m=m, pgx=pgx: e.matmul(pgx[:, :], lhsT=lwb[:, 4 + m, :], rhs=xclb[:, :], start=True, stop=True),
                             r=["xclb"], w=[pgx.name])
                        P.act(lambda e, m=m, pga=pga: e.activation(out=lr[:, :], in_=pga[:, :], func=AF.Tanh, scale=0.5,
                                                                   bias=small[:, 44 + m:45 + m]), r=[pga.name], w=["lr"])
                        P.act(lambda e, m=m, pgx=pgx: e.activation(out=li[:, :], in_=pgx[:, :], func=AF.Tanh, scale=0.5,
                                                                   bias=small[:, 48 + m:49 + m]), r=[pgx.name], w=["li"])
                        P.act(lambda e, m=m: e.activation(out=la[:, :], in_=lr[:, :], func=AF.Exp, scale=small[:, 12 + m:13 + m],
                                                          bias=small[:, 12 + m:13 + m]), r=["lr"], w=["la"])
                        P.act(lambda e, m=m: e.activation(out=lm[:, :], in_=lr[:, :], func=AF.Exp, scale=small[:, 4 + m:5 + m],
                                                          bias=small[:, 4 + m:5 + m]), r=["lr"], w=["lm"])
                        P.dve(lambda e: e.tensor_scalar(out=lm[:, :], in0=lm[:, :], scalar1=0.9999999, scalar2=-1.0, op0=ALU.min, op1=ALU.mult),
                              r=["lm"], w=["lm"])
                        P.act(lambda e: e.activation(out=lm[:, :], in_=lm[:, :], func=AF.Ln, bias=1.0), r=["lm"], w=["lm"])
                        P.act(lambda e: e.activation(out=lm[:, :], in_=lm[:, :], func=AF.Exp, scale=0.5), r=["lm"], w=["lm"])
                        if T == 0:
                            P.dve(lambda e: e.memset(lm[:, 0:1], 1.0), r=["lm"], w=["lm"])
                        P.dve(lambda e: e.scalar_tensor_tensor(out=li[:, :], in0=li[:, :], scalar=1.0, in1=xcl[:, :],
                                                               op0=ALU.add, op1=ALU.mult), r=["li", "xcl"], w=["li"])
                        P.dve(lambda e: e.scalar_tensor_tensor(out=li[:, :], in0=li[:, :], scalar=0.5, in1=lm[:, :],
                                                               op0=ALU.mult, op1=ALU.mult), r=["li", "lm"], w=["li"])
                        hc = hsb[hcount % 2]
                        hcount += 1
                        P.dve(lambda e, m=m, hc=hc: e.tensor_tensor_scan(
                            out=hc[:, :], data0=la[:, :], data1=li[:, :], initial=hcar[:, m:m + 1],
                            op0=ALU.mult, op1=ALU.add), r=["la", "li", "hcar"], w=[hc.name])
                        P.dve(lambda e, m=m, hc=hc: e.tensor_copy(out=hcar[:, m:m + 1], in_=hc[:, 511:512]), r=[hc.name], w=["hcar"])
                        P.dve(lambda e, m=m, hc=hc: e.tensor_tensor(out=mixT[:, 8 + m, :], in0=hc[:, :], in1=zl[:, m, :], op=ALU.mult),
                               r=[hc.name, "zl"], w=["mixT"])
                    for c in range(4):
                        for hf in range(2):
                            pt = pring[gi % 4]
                            gi += 1

                            def zproj(e, pt=pt, c=c, hf=hf):
                                ins = None
                                for k in range(8):
                                    ins = e.matmul(pt[:, :], lhsT=uTb[:, k, 128 * c:128 * c + 128],
                                                   rhs=WB[:, k, 512 * hf:512 * hf + 512], start=(k == 0), stop=(k == 7))
                                return ins
                            P.pe(zproj, r=WBt + ["uT"], w=[pt.name])
                            P.act(lambda e, pt=pt, hf=hf, c=c: e.activation(out=zsd[:, c, 512 * hf:512 * hf + 512], in_=pt[:, :],
                                                                            func=AF.Silu), r=[pt.name], w=["zsd%d" % c])
                    for c in range(4):
                        cs = slice(128 * c, 128 * c + 128)
                        Mt = Mt2[c % 2]
                        xdt = xdt2[c % 2]
                        xdd = xdd2[c % 2]
                        xDs = xDs2[c % 2]

                        def trx(e, cs=cs):
                            ins = None
                            for g in range(8):
                                ins = e.transpose(out=ptr[:, g, :], in_=xc[:, g, cs], identity=identb)
                            return ins
                        P.pe(trx, r=["xc"], w=["ptr"])
                        P.dve(lambda e: e.tensor_copy(out=xs_tok[:, :].rearrange("p (g c) -> p g c", g=8), in_=ptr[:, :, :]),
                              r=["ptr"], w=["xs_tok"])

                        def trb(e, cs=cs):
                            e.transpose(out=ptr[:, 0, :], in_=xc[:, 8, cs], identity=identb)
                            return e.transpose(out=ptr[:, 1, :], in_=xc[:, 9, cs], identity=identb)
                        P.pe(trb, r=["xc"], w=["ptr"])
                        P.dve(lambda e: e.tensor_copy(out=b_tok[:, :, :], in_=ptr[:, 0:2, :]), r=["ptr"], w=["b_tok"])

                        def cbm(e, cs=cs):
                            e.matmul(pm[:, 256:384], lhsT=xc[:, 8, cs], rhs=xc[:, 10, cs], start=True, stop=True)
                            return e.matmul(pm[:, 384:512], lhsT=xc[:, 9, cs], rhs=xc[:, 11, cs], start=True, stop=True)
                        P.pe(cbm, r=["xc"], w=["pm"])
                        P.act(lambda e: e.activation(out=cb[:, :, :], in_=pm[:, 256:512].rearrange("p (g c) -> p g c", g=2),
                                                     func=AF.Copy), r=["pm"], w=["cb"])
                        hs3 = xs_tok[:, :].rearrange("p (h c) -> p h c", h=16)
                        P.dve(lambda e, c=c, hs3=hs3, xdt=xdt: e.tensor_tensor(
                            out=xdt[:, :].rearrange("p (h c) -> p h c", h=16), in0=hs3,
                            in1=tok3[:, c, :].unsqueeze(2).to_broadcast([128, 16, 64]), op=ALU.mult),
                            r=["xs_tok", "tok3"], w=[xdt.name])
                        P.dve(lambda e, c=c, hs3=hs3, xdd=xdd: e.tensor_tensor(
                            out=xdd[:, :].rearrange("p (h c) -> p h c", h=16), in0=hs3,
                            in1=wst[:, c, :].unsqueeze(2).to_broadcast([128, 16, 64]), op=ALU.mult),
                            r=["xs_tok", "wst"], w=[xdd.name])
                        P.dve(lambda e, hs3=hs3, xDs=xDs: e.tensor_tensor(
                            out=xDs[:, :].rearrange("p (h c) -> p h c", h=16), in0=hs3,
                            in1=Dbc.unsqueeze(2).to_broadcast([128, 16, 64]), op=ALU.mult),
                            r=["xs_tok"], w=[xDs.name])
                        for hg in range(4):
                            ltile = Lt[hg % 2]

                            pz, pzt = ((pa[0], pa[0].name), (ptrf, "ptr"))[hg % 2]

                            def zmm(e, hg=hg, cs=cs, pz=pz):
                                for q_ in range(4):
                                    h = 4 * hg + q_
                                    e.matmul(pz[:, 128 * q_:128 * q_ + 128], lhsT=selR[:, h, :], rhs=D2[:, cs],
                                             start=(q_ == 0), stop=False)
                                    e.matmul(pz[:, 128 * q_:128 * q_ + 128], lhsT=nD2[:, cs], rhs=selR[:, h, :],
                                             start=False, stop=False)
                                return e.matmul(pz[:, :], lhsT=identb, rhs=mneg4, start=False, stop=True)
                            P.pe(zmm, r=["D2", "nD2"], w=[pzt])
                            P.act(lambda e, ltile=ltile, pz=pz: e.activation(out=ltile[:, :], in_=pz[:, :], func=AF.Exp),
                                  r=[pzt], w=[ltile.name])
                            g = hg // 2
                            P.dve(lambda e, hg=hg, g=g, ltile=ltile, Mt=Mt: e.tensor_tensor(
                                out=Mt[:, 4 * hg:4 * hg + 4, :], in0=ltile[:, :].rearrange("p (q c) -> p q c", q=4),
                                in1=cb[:, g, :].unsqueeze(1).to_broadcast([128, 4, 128]), op=ALU.mult),
                                r=[ltile.name, "cb"], w=[Mt.name])
                        for g in range(2):
                            gs = slice(512 * g, 512 * g + 512)
                            n2 = (2 * c + g) % 2
                            pyd = (pa[1], pj[0])[n2]
                            pyo = (pa[2], pj[1])[n2]
                            ysum = ysum2[n2]
                            gated = gated2[n2]
                            ysn = ysn2[n2]

                            def ydm(e, g=g, gs=gs, pyd=pyd, Mt=Mt, xdt=xdt, xDs=xDs):
                                for q_ in range(8):
                                    h = 8 * g + q_
                                    e.matmul(pyd[:, 64 * q_:64 * q_ + 64], lhsT=Mt[:, h, :], rhs=xdt[:, 64 * h:64 * h + 64],
                                             start=(q_ == 0), stop=False)
                                return e.matmul(pyd[:, :], lhsT=identb, rhs=xDs[:, gs], start=False, stop=True)
                            P.pe(ydm, r=[xDs.name, Mt.name, xdt.name], w=[pyd.name])
                            P.pe(lambda e, g=g, cs=cs, pyo=pyo: e.matmul(pyo[:, :], lhsT=xc[:, 10 + g, cs], rhs=prevb[:, g, :],
                                                                         start=True, stop=True), r=["xc", "prevb%d" % g], w=[pyo.name])
                            P.pe(lambda e, g=g, gs=gs, xdd=xdd: e.matmul(pst[:, :], lhsT=b_tok[:, g, :], rhs=xdd[:, gs],
                                                                         start=True, stop=True), r=["b_tok", xdd.name], w=[pst.name])
                            el_b = eal_tok[:, c, 8 * g:8 * g + 8].unsqueeze(2).to_broadcast([128, 8, 64])
                            P.dve(lambda e, g=g, el_b=el_b: e.tensor_tensor(
                                out=prev[:, g, :].rearrange("p (h c) -> p h c", h=8), in0=prev[:, g, :].rearrange("p (h c) -> p h c", h=8),
                                in1=el_b, op=ALU.mult), r=["prev%d" % g, "prevb%d" % g, "eal_tok"], w=["prev%d" % g])
                            P.dve(lambda e, g=g: e.tensor_tensor(out=prev[:, g, :], in0=pst[:, :], in1=prev[:, g, :], op=ALU.add),
                                  r=[pst.name, "prev%d" % g], w=["prev%d" % g])
                            P.act(lambda e, g=g: e.activation(out=prevb[:, g, :], in_=prev[:, g, :], func=AF.Copy),
                                  r=["prev%d" % g], w=["prevb%d" % g])
                            ea_b = ea_tok[:, c, 8 * g:8 * g + 8].unsqueeze(2).to_broadcast([128, 8, 64])
                            P.dve(lambda e, ea_b=ea_b, ysum=ysum, pyo=pyo: e.tensor_tensor(
                                out=ysum[:, :].rearrange("p (h c) -> p h c", h=8), in0=pyo[:, :].rearrange("p (h c) -> p h c", h=8),
                                in1=ea_b, op=ALU.mult), r=[pyo.name, "ea_tok"], w=[ysum.name])
                            P.dve(lambda e, ysum=ysum, pyd=pyd: e.tensor_tensor(out=ysum[:, :], in0=pyd[:, :], in1=ysum[:, :], op=ALU.add),
                                  r=[pyd.name, ysum.name], w=[ysum.name])
                            P.dve(lambda e, gs=gs, c=c, ysum=ysum, gated=gated: e.tensor_tensor(out=gated[:, :], in0=ysum[:, :], in1=zsd[:, c, gs], op=ALU.mult),
                                  r=[ysum.name, "zsd%d" % c], w=[gated.name])
                            q0 = 32 + 4 * n2
                            P.act(lambda e, gated=gated, q0=q0: e.activation(out=junkb[:, :], in_=gated[:, :], func=AF.Square,
                                                                             accum_out=small[:, q0:q0 + 1]), r=[gated.name], w=["junk", "ssq%d" % n2])
                            P.act(lambda e, q0=q0: e.activation(out=small[:, q0 + 1:q0 + 2], in_=small[:, q0:q0 + 1], func=AF.Ln,
                                                                scale=1.0 / 512, bias=epsc), r=["ssq%d" % n2], w=["ssq1%d" % n2])
                            P.act(lambda e, q0=q0: e.activation(out=small[:, q0 + 2:q0 + 3], in_=small[:, q0 + 1:q0 + 2], func=AF.Exp, scale=-0.5),
                                  r=["ssq1%d" % n2], w=["ssq2%d" % n2])
                            P.dve(lambda e, gs=gs, ysn=ysn, gated=gated, q0=q0: e.scalar_tensor_tensor(
                                out=ysn[:, :], in0=gated[:, :], scalar=small[:, q0 + 2:q0 + 3], in1=gnb[:, gs], op0=ALU.mult, op1=ALU.mult),
                                r=[gated.name, "ssq2%d" % n2], w=[ysn.name])

                            def try_(e, ysn=ysn):
                                ins = None
                                for q_ in range(4):
                                    ins = e.transpose(out=pmb[:, q_, :], in_=ysn[:, 128 * q_:128 * q_ + 128], identity=identb)
                                return ins
                            P.pe(try_, r=[ysn.name], w=["pm"])
                            P.dve(lambda e, g=g, cs=cs: e.tensor_copy(out=mixT[:, 4 * g:4 + 4 * g, cs], in_=pmb[:, 0:4, :]),
                                  r=["pm"], w=["mixT"])
                    P.dma("sp", lambda e, t0=t0: e.dma_start(out=ymd_d[:, :, t0:t0 + 512], in_=mixT[:, :, :]),
                          r=["mixT"], w=["ymd"])
                    if l == 0 and T == 0:
                        dump("mixT", mixT[:, :, :], [128, 12, 512], BF16, ["mixT"])
                    if l == 0 and T == 1:
                        dump("mixT1", mixT[:, :, :], [128, 12, 512], BF16, ["mixT"])
                P.barrier()

            with contextlib.ExitStack() as sC:
                sb = mk(sC, "_l%d" % l)
                WO = sb("WO", [128, 16, 1024], BF16)
                mixC = [sb("mixC%d" % i, [128, 16, 512], BF16) for i in range(2)]
                hxc = [sb("hxc%d" % i, [128, 4, 1024], F32) for i in range(2)]
                hout = [sb("hout%d" % i, [128, 1024], F32) for i in range(2)]
                for k in range(0, 16, 4):
                    P.dma("sp", lambda e, k=k: e.dma_start(out=WO[:, k:k + 4, :], in_=wobf_d[l][:, k:k + 4, :]), w=["WO"])
                gi = 0
                for T in range(nchunks):
                    t0 = 512 * T
                    mc = mixC[T % 2]
                    hx = hxc[T % 2]
                    P.dma("sp", lambda e, t0=t0, mc=mc: e.dma_start(out=mc[:, 0:4, :], in_=yatd_d[:, :, t0:t0 + 512]), w=[mc.name + "a"])
                    P.dma("sp", lambda e, t0=t0, mc=mc: e.dma_start(out=mc[:, 4:16, :], in_=ymd_d[:, :, t0:t0 + 512]), w=[mc.name + "b"])
                    P.dma("sp", lambda e, t0=t0, hx=hx: e.dma_start(
                        out=hx[:, :, :], in_=src_d[t0:t0 + 512, :].rearrange("(i p) d -> p i d", p=128)), w=[hx.name])
                    for i in range(4):
                        ho = hout[i % 2]
                        for hf in range(2):
                            pt = pj[gi % 2]
                            gi += 1

                            def oproj(e, pt=pt, i=i, hf=hf, mc=mc):
                                ins = None
                                for k in range(16):
                                    ins = e.matmul(pt[:, :], lhsT=mc[:, k, 128 * i:128 * i + 128],
                                                   rhs=WO[:, k, 512 * hf:512 * hf + 512], start=(k == 0), stop=(k == 15))
                                return ins
                            P.pe(oproj, r=["WO", mc.name + "a", mc.name + "b"], w=[pt.name])
                            P.dve(lambda e, pt=pt, i=i, hf=hf, ho=ho, hx=hx: e.tensor_tensor(
                                out=ho[:, 512 * hf:512 * hf + 512], in0=pt[:, :], in1=hx[:, i, 512 * hf:512 * hf + 512], op=ALU.add),
                                r=[pt.name, hx.name], w=[ho.name])
                        r0 = t0 + 128 * i
                        P.dma("sp", lambda e, ho=ho, r0=r0: e.dma_start(out=dst_d[r0:r0 + 128, :], in_=ho[:, :]),
                              r=[ho.name], w=["dst"])
                P.barrier()

        for l in range(L):
            do_layer(l)
        P.emit()
        stats = P.stats
    return nc, dbg_out, stats


_ML = None


def _consts():
    cc = np.zeros((128, NCC), np.float32)
    cc[:, CC_ID:CC_ID + 128] = np.eye(128, dtype=np.float32)
    s_ = np.arange(128)[:, None]
    t_ = np.arange(128)[None, :]
    cc[:, CC_TRI:CC_TRI + 128] = (s_ <= t_).astype(np.float32)
    cc[:, CC_MNEG:CC_MNEG + 512] = np.tile(np.where(s_ > t_, NEG, 0.0).astype(np.float32), (1, 4))
    bd = np.zeros((128, 128), np.float32)
    bd[0:64, 0:64] = 1.0 / 64
    bd[64:128, 64:128] = 1.0 / 64
    cc[:, CC_BD:CC_BD + 128] = bd
    for h in range(8):
        cc[h, CC_SELH + 128 * h:CC_SELH + 128 * h + 128] = 1.0
    for h in range(16):
        cc[h, CC_SELR + 128 * h:CC_SELR + 128 * h + 128] = 1.0
        cc[32 + h, CC_SELR + 128 * h:CC_SELR + 128 * h + 128] = 1.0
    return cc


def _pack(inputs, L):
    f = lambda a: np.asarray(a, np.float32)
    pp = np.zeros((L, 128, NPP), np.float32)
    lw = np.zeros((L, 128, 8, 128), np.float32)
    for l in range(L):
        pp[l, :, PP_G:PP_G + 8] = f(inputs["norm_g"])[l].reshape(8, 128).T
        pp[l, :, PP_GQ] = np.tile(f(inputs["q_norm_g"])[l], 2)
        pp[l, :, PP_GK] = np.tile(f(inputs["k_norm_g"])[l], 2)
        pp[l, 0:8, PP_FB] = f(inputs["forget_b"])[l]
        pp[l, 0:16, PP_DTB] = f(inputs["ssd_dt_bias"])[l]
        pp[l, 0:16, PP_ALOG] = f(inputs["ssd_a_log"])[l]
        pp[l, :, PP_BA:PP_BA + 4] = f(inputs["lru_b_a"])[l].reshape(4, 128).T
        pp[l, :, PP_BX:PP_BX + 4] = f(inputs["lru_b_x"])[l].reshape(4, 128).T
        pp[l, :, PP_LAM:PP_LAM + 4] = f(inputs["lru_lambda"])[l].reshape(4, 128).T
        scw = f(inputs["ssd_conv_w"])[l]
        pp[l, :, PP_SCW:PP_SCW + 48] = scw.T.reshape(12, 128, 4).transpose(1, 0, 2).reshape(128, 48)
        pp[l, :, PP_SCB:PP_SCB + 12] = f(inputs["ssd_conv_b"])[l].reshape(12, 128).T
        lcw = f(inputs["lru_conv_w"])[l]
        pp[l, :, PP_LCW:PP_LCW + 16] = lcw.T.reshape(4, 128, 4).transpose(1, 0, 2).reshape(128, 16)
        pp[l, :, PP_LCB:PP_LCB + 4] = f(inputs["lru_conv_b"])[l].reshape(4, 128).T
        pp[l, :, PP_DBC:PP_DBC + 16] = f(inputs["ssd_d"])[l][None, :]
        pp[l, :, PP_GN:PP_GN + 1024] = f(inputs["ssd_norm_g"])[l][None, :]
        for gate, nm in enumerate(("lru_w_a", "lru_w_x")):
            w = f(inputs[nm])[l]
            for n in range(8):
                m, half = n // 2, n % 2
                lw[l, 64 * half:64 * half + 64, 4 * gate + m, 64 * half:64 * half + 64] = w[n]
    return pp, lw


_CACHE = {}


def kernel(**inputs):
    L = 2
    x = np.ascontiguousarray(np.asarray(inputs["x"], np.float32))
    B = x.shape[0]
    if "nc" not in _CACHE:
        _CACHE["nc"] = build_program(L)[0]
    nc = _CACHE["nc"]
    pp, lw = _pack(inputs, L)
    cc = _consts()
    w_in = np.ascontiguousarray(np.asarray(inputs["w_in"], np.float32))
    w_out = np.ascontiguousarray(np.asarray(inputs["w_out"], np.float32))
    in_maps = [{"x": x[b], "w_in": w_in, "w_out": w_out, "pp": pp, "lw": lw, "cc": cc} for b in range(B)]
    res = run_bass_kernel_spmd(nc, in_maps, core_ids=list(range(B)))
    return np.stack([np.asarray(r["out"], np.float32) for r in res.results], axis=0)
```
